# Optimizing a Trainium2 kernel written in Bass

```python
import jax, jax.numpy as jnp
from jax import lax
import numpy as np

D_MODEL = 2048
BATCH = 2
SEQ = 8192
DEPTH = 4

N_MEM = 256
N_A_LAYERS = DEPTH // 2
N_B_LAYERS = DEPTH - N_A_LAYERS
EPS = 1e-6

MEM_HEADS = 4
MEM_WIDTH = D_MODEL // 4
MEM_HEAD_DIM = MEM_WIDTH // MEM_HEADS
MAIN_WIDTH = D_MODEL - MEM_WIDTH

MLSTM_HEADS = 4
MLSTM_V_DIM = MAIN_WIDTH // MLSTM_HEADS
MLSTM_QK_DIM = MLSTM_V_DIM // 2
MLSTM_CHUNK = 64
CONV_WIDTH = 4

MLA_NOPE_DIM = 128
MLA_ROPE_DIM = 64
MLA_QK_DIM = MLA_NOPE_DIM + MLA_ROPE_DIM
MLA_V_DIM = 128
MLA_HEADS = MAIN_WIDTH // MLA_V_DIM
Q_LORA_RANK = 448
KV_LORA_RANK = 512
ROPE_THETA = 10000.0
Q_BLOCK = 128

D_FF = 5632

A_COLS = (2 * MLSTM_HEADS * MLSTM_QK_DIM, MAIN_WIDTH, MAIN_WIDTH, 2 * MLSTM_HEADS, MEM_WIDTH)
B_COLS = (Q_LORA_RANK, MEM_WIDTH)

F32 = jnp.float32

kernel_name = 'yoco_mlstm_mla_macaron_memory'


def _split(t, sizes):
    idx = [int(i) for i in np.cumsum(sizes)[:-1]]
    return jnp.split(t, idx, axis=-1)


def _rms_norm(t, gain):
    tf = t.astype(F32)
    y = tf * lax.rsqrt(jnp.mean(tf * tf, axis=-1, keepdims=True) + EPS)
    return (y * gain.astype(F32)).astype(t.dtype)


def _swiglu(h, w_in, w_out):
    g, u = jnp.split(h @ w_in, 2, axis=-1)
    return (jax.nn.silu(g) * u) @ w_out


def _rope_tables(positions):
    inv_freq = ROPE_THETA ** (-jnp.arange(0, MLA_ROPE_DIM, 2, dtype=F32) / MLA_ROPE_DIM)
    ang = positions.astype(F32)[..., None] * inv_freq
    return jnp.cos(ang), jnp.sin(ang)


def _rope(t, cos, sin):
    x1, x2 = jnp.split(t.astype(F32), 2, axis=-1)
    c = cos[:, :, None, :]
    s = sin[:, :, None, :]
    return jnp.concatenate([x1 * c - x2 * s, x2 * c + x1 * s], axis=-1).astype(t.dtype)


def _causal_conv(t, w):
    return lax.conv_general_dilated(t, w[:, None, :], window_strides=(1,), padding=[(CONV_WIDTH - 1, 0)],
                                    dimension_numbers=('NWC', 'WIO', 'NWC'), feature_group_count=t.shape[-1])


def _to_chunks(t):
    b, s, h = t.shape[:3]
    t = t.reshape((b, s // MLSTM_CHUNK, MLSTM_CHUNK, h) + t.shape[3:])
    return jnp.moveaxis(t, (1, 3), (0, 2))


def _from_chunks(t):
    t = jnp.moveaxis(t, (0, 2), (1, 3))
    b, nc, l, h, d = t.shape
    return t.reshape(b, nc * l, h, d)


def _mlstm_chunk_step(carry, xs):
    c_state, n_state, m_state = carry
    q, k, v, ig, lf = xs
    L = q.shape[2]
    causal = jnp.tril(jnp.ones((L, L), dtype=bool))
    b = jnp.cumsum(lf, axis=-1)
    log_w = jnp.where(causal, b[..., :, None] - b[..., None, :] + ig[..., None, :], -jnp.inf)
    log_inter = b + m_state[..., None]
    m_t = jnp.maximum(jnp.max(log_w, axis=-1), log_inter)
    w = jnp.exp(log_w - m_t[..., None])
    w_inter = jnp.exp(log_inter - m_t)
    s = jnp.einsum('bhtd,bhsd->bhts', q, k) * w
    num = jnp.einsum('bhts,bhsv->bhtv', s, v) + w_inter[..., None] * jnp.einsum('bhtd,bhdv->bhtv', q, c_state)
    den = jnp.sum(s, axis=-1) + w_inter * jnp.einsum('bhtd,bhd->bht', q, n_state)
    h = num / jnp.maximum(jnp.abs(den), jnp.exp(-m_t))[..., None]
    g = b[..., -1]
    log_a = g[..., None] - b + ig
    m_new = jnp.maximum(g + m_state, jnp.max(log_a, axis=-1))
    decay = jnp.exp(g + m_state - m_new)
    a = jnp.exp(log_a - m_new[..., None])
    c_new = decay[..., None, None] * c_state + jnp.einsum('bhs,bhsd,bhsv->bhdv', a, k, v)
    n_new = decay[..., None] * n_state + jnp.einsum('bhs,bhsd->bhd', a, k)
    return (c_new, n_new, m_new), h


def _mlstm_branch(h, w_in, b_gates, conv_w, head_gain):
    bsz, seq, _ = h.shape
    qk, v, o, gates, mq = _split(h @ w_in, A_COLS)
    qk = jax.nn.silu(_causal_conv(qk, conv_w))
    q, k = jnp.split(qk, 2, axis=-1)
    q = q.reshape(bsz, seq, MLSTM_HEADS, MLSTM_QK_DIM).astype(F32) * (MLSTM_QK_DIM ** -0.5)
    k = k.reshape(bsz, seq, MLSTM_HEADS, MLSTM_QK_DIM).astype(F32)
    v = v.reshape(bsz, seq, MLSTM_HEADS, MLSTM_V_DIM).astype(F32)
    gates = gates.astype(F32) + b_gates.astype(F32)
    ig, fg = jnp.split(gates, 2, axis=-1)
    lf = jax.nn.log_sigmoid(fg)
    init = (jnp.zeros((bsz, MLSTM_HEADS, MLSTM_QK_DIM, MLSTM_V_DIM), F32),
            jnp.zeros((bsz, MLSTM_HEADS, MLSTM_QK_DIM), F32),
            jnp.zeros((bsz, MLSTM_HEADS), F32))
    xs = (_to_chunks(q), _to_chunks(k), _to_chunks(v), _to_chunks(ig), _to_chunks(lf))
    _, hs = lax.scan(_mlstm_chunk_step, init, xs)
    hs = _rms_norm(_from_chunks(hs), head_gain.reshape(MLSTM_HEADS, MLSTM_V_DIM))
    out = jax.nn.sigmoid(o.astype(F32)) * hs.reshape(bsz, seq, MAIN_WIDTH)
    return out.astype(h.dtype), mq


def _causal_block_attention(q, k, v):
    bsz, seq, nh, dqk = q.shape
    nblk = seq // Q_BLOCK
    scale = dqk ** -0.5
    qb = jnp.moveaxis(q.reshape(bsz, nblk, Q_BLOCK, nh, dqk), 1, 0)
    starts = jnp.arange(nblk, dtype=jnp.int32) * Q_BLOCK
    k_pos = jnp.arange(seq, dtype=jnp.int32)

    def one_block(args):
        q_blk, start = args
        s = jnp.einsum('bqhd,bkhd->bhqk', q_blk, k).astype(F32) * scale
        q_pos = start + jnp.arange(Q_BLOCK, dtype=jnp.int32)
        s = jnp.where(k_pos[None, :] <= q_pos[:, None], s, -jnp.inf)
        p = jax.nn.softmax(s, axis=-1).astype(v.dtype)
        return jnp.einsum('bhqk,bkhd->bqhd', p, v)

    out = lax.map(one_block, (qb, starts))
    return jnp.moveaxis(out, 0, 1).reshape(bsz, seq, nh, v.shape[-1])


def _shared_kv(x, kv_gain, w_dkv, kv_latent_gain, w_ukv, k_gain, cos, sin):
    bsz, seq, _ = x.shape
    c_kv, k_pe = jnp.split(_rms_norm(x, kv_gain) @ w_dkv, [KV_LORA_RANK], axis=-1)
    kv = (_rms_norm(c_kv, kv_latent_gain) @ w_ukv).reshape(bsz, seq, MLA_HEADS, MLA_NOPE_DIM + MLA_V_DIM)
    k_nope, v = jnp.split(kv, [MLA_NOPE_DIM], axis=-1)
    k_pe = jnp.broadcast_to(k_pe[:, :, None, :], (bsz, seq, MLA_HEADS, MLA_ROPE_DIM))
    k = _rms_norm(jnp.concatenate([k_nope, k_pe], axis=-1), k_gain)
    k = jnp.concatenate([k[..., :MLA_NOPE_DIM], _rope(k[..., MLA_NOPE_DIM:], cos, sin)], axis=-1)
    return k, v


def _mla_branch(h, w_in, q_latent_gain, w_uq, q_gain, k_sh, v_sh, cos, sin):
    bsz, seq, _ = h.shape
    cq, mq = _split(h @ w_in, B_COLS)
    q = (_rms_norm(cq, q_latent_gain) @ w_uq).reshape(bsz, seq, MLA_HEADS, MLA_QK_DIM)
    q = _rms_norm(q, q_gain)
    q = jnp.concatenate([q[..., :MLA_NOPE_DIM], _rope(q[..., MLA_NOPE_DIM:], cos, sin)], axis=-1)
    o = _causal_block_attention(q, k_sh, v_sh)
    return o.reshape(bsz, seq, MAIN_WIDTH), mq


def _memory_kv(mem, mem_gain, w_mem_kv, k_gain):
    bsz, nm, _ = mem.shape
    mk, mv = jnp.split(_rms_norm(mem, mem_gain) @ w_mem_kv, 2, axis=-1)
    mk = _rms_norm(mk.reshape(bsz, nm, MEM_HEADS, MEM_HEAD_DIM), k_gain)
    mv = mv.reshape(bsz, nm, MEM_HEADS, MEM_HEAD_DIM)
    return mk, mv


def _memory_attention(mq, mk, mv, q_gain):
    bsz, seq, _ = mq.shape
    q = _rms_norm(mq.reshape(bsz, seq, MEM_HEADS, MEM_HEAD_DIM), q_gain)
    s = jnp.einsum('bshd,bmhd->bhsm', q, mk).astype(F32) * (MEM_HEAD_DIM ** -0.5)
    p = jax.nn.softmax(s, axis=-1).astype(mv.dtype)
    return jnp.einsum('bhsm,bmhd->bshd', p, mv).reshape(bsz, seq, MEM_WIDTH)


def setup_inputs(seed: int = 0) -> dict:
    key = jax.random.key(seed)
    k = jax.random.split(key, 32)

    def w(i, shape, fan_in):
        return jax.random.normal(k[i], shape, F32) * (fan_in ** -0.5)

    def gain(i, shape):
        return 1.0 + 0.02 * jax.random.normal(k[i], shape, F32)

    a_cols = int(sum(A_COLS))
    b_cols = int(sum(B_COLS))
    offsets = jax.random.randint(k[2], (BATCH, 1), 0, 1024, dtype=jnp.int32)
    positions = offsets + jnp.arange(SEQ, dtype=jnp.int32)[None, :]
    i_bias = 0.1 * jax.random.normal(k[13], (N_A_LAYERS, MLSTM_HEADS), F32)
    f_bias = jnp.linspace(3.0, 6.0, MLSTM_HEADS, dtype=F32)[None, :] + 0.1 * jax.random.normal(k[14], (N_A_LAYERS, MLSTM_HEADS), F32)
    return {
        'x': jax.random.normal(k[0], (BATCH, SEQ, D_MODEL), F32),
        'mem': jax.random.normal(k[1], (BATCH, N_MEM, D_MODEL), F32),
        'positions': positions,
        'ffn1_gain': gain(3, (DEPTH, D_MODEL)),
        'ffn1_w_in': w(4, (DEPTH, D_MODEL, 2 * D_FF), D_MODEL),
        'ffn1_w_out': w(5, (DEPTH, D_FF, D_MODEL), D_FF),
        'mix_gain': gain(6, (DEPTH, D_MODEL)),
        'w_out': w(7, (DEPTH, D_MODEL, D_MODEL), D_MODEL),
        'mem_gain': gain(8, (DEPTH, D_MODEL)),
        'w_mem_kv': w(9, (DEPTH, D_MODEL, 2 * MEM_WIDTH), D_MODEL),
        'mem_q_gain': gain(10, (DEPTH, MEM_HEAD_DIM)),
        'mem_k_gain': gain(11, (DEPTH, MEM_HEAD_DIM)),
        'a_w_in': w(12, (N_A_LAYERS, D_MODEL, a_cols), D_MODEL),
        'a_b_gates': jnp.concatenate([i_bias, f_bias], axis=-1),
        'a_conv': w(15, (N_A_LAYERS, CONV_WIDTH, 2 * MLSTM_HEADS * MLSTM_QK_DIM), CONV_WIDTH),
        'a_head_gain': gain(16, (N_A_LAYERS, MAIN_WIDTH)),
        'kv_gain': gain(17, (D_MODEL,)),
        'w_dkv': w(18, (D_MODEL, KV_LORA_RANK + MLA_ROPE_DIM), D_MODEL),
        'kv_latent_gain': gain(19, (KV_LORA_RANK,)),
        'w_ukv': w(20, (KV_LORA_RANK, MLA_HEADS * (MLA_NOPE_DIM + MLA_V_DIM)), KV_LORA_RANK),
        'k_gain': gain(21, (MLA_QK_DIM,)),
        'b_w_in': w(22, (N_B_LAYERS, D_MODEL, b_cols), D_MODEL),
        'b_q_latent_gain': gain(23, (N_B_LAYERS, Q_LORA_RANK)),
        'b_w_uq': w(24, (N_B_LAYERS, Q_LORA_RANK, MLA_HEADS * MLA_QK_DIM), Q_LORA_RANK),
        'b_q_gain': gain(25, (N_B_LAYERS, MLA_QK_DIM)),
        'ffn2_gain': gain(26, (DEPTH, D_MODEL)),
        'ffn2_w_in': w(27, (DEPTH, D_MODEL, 2 * D_FF), D_MODEL),
        'ffn2_w_out': w(28, (DEPTH, D_FF, D_MODEL), D_FF),
    }


def reference(x, mem, positions, ffn1_gain, ffn1_w_in, ffn1_w_out, mix_gain, w_out, mem_gain, w_mem_kv,
              mem_q_gain, mem_k_gain, a_w_in, a_b_gates, a_conv, a_head_gain, kv_gain, w_dkv, kv_latent_gain,
              w_ukv, k_gain, b_w_in, b_q_latent_gain, b_w_uq, b_q_gain, ffn2_gain, ffn2_w_in, ffn2_w_out):
    cos, sin = _rope_tables(positions)
    k_sh = None
    v_sh = None
    for layer in range(DEPTH):
        x = x + 0.5 * _swiglu(_rms_norm(x, ffn1_gain[layer]), ffn1_w_in[layer], ffn1_w_out[layer])
        h = _rms_norm(x, mix_gain[layer])
        if layer < N_A_LAYERS:
            a = layer
            main, mq = _mlstm_branch(h, a_w_in[a], a_b_gates[a], a_conv[a], a_head_gain[a])
        else:
            j = layer - N_A_LAYERS
            main, mq = _mla_branch(h, b_w_in[j], b_q_latent_gain[j], b_w_uq[j], b_q_gain[j], k_sh, v_sh, cos, sin)
        mk, mv = _memory_kv(mem, mem_gain[layer], w_mem_kv[layer], mem_k_gain[layer])
        mem_out = _memory_attention(mq, mk, mv, mem_q_gain[layer])
        x = x + jnp.concatenate([main, mem_out], axis=-1) @ w_out[layer]
        x = x + 0.5 * _swiglu(_rms_norm(x, ffn2_gain[layer]), ffn2_w_in[layer], ffn2_w_out[layer])
        if layer == N_A_LAYERS - 1:
            k_sh, v_sh = _shared_kv(x, kv_gain, w_dkv, kv_latent_gain, w_ukv, k_gain, cos, sin)
    return x
```

```python
import numpy as np
from contextlib import ExitStack
import concourse.bass as bass
import concourse.mybir as mybir
from concourse.bass_utils import run_bass_kernel_spmd

F32 = mybir.dt.float32
F32R = mybir.dt.float32r
AF = mybir.ActivationFunctionType
ALU = mybir.AluOpType

D = 2048
DFF = 5632
KC = 16
FC = 44
TT = 512
EPS = 1e-6
SEQ = 8192
NTOK = 2048
NCORES = 8
MASKV = -30000.0

SAME_ENGINE_SYNC = True


class Prog:
    ENGS = ("pe", "act", "dve", "pool", "sp")

    def __init__(self, nc):
        self.nc = nc
        self.ops = []
        self.nops = 0
        self.res_w = {}
        self.res_r = {}
        self.info = {}
        self.esem = {e: nc.alloc_semaphore(name=f"sem_{e}") for e in self.ENGS}
        self.ecount = {e: 0 for e in self.ENGS}
        self.dsem = {}
        self.dcount = {}
        self.waited = {e: {} for e in self.ENGS}

    def _skip(self, di, eng):
        return di["dma"] is None and di["eng"] == eng and (eng == "pe" or not SAME_ENGINE_SYNC)

    def op(self, eng, fn, reads=(), writes=(), dma=None):
        deps = set()
        for r in reads:
            deps.update(self.res_w.get(r, ()))
        for r in writes:
            deps.update(self.res_w.get(r, ()))
            rr = self.res_r.get(r)
            if rr:
                deps.update(rr[0].values())
                deps.update(rr[1])
        oid = self.nops
        self.nops += 1
        self.info[oid] = {"eng": eng, "dma": dma, "needs": False, "ev": None}
        for d in deps:
            di = self.info[d]
            if not self._skip(di, eng):
                di["needs"] = True
        self.ops.append((oid, eng, fn, sorted(deps), dma))
        for r in reads:
            rr = self.res_r.setdefault(r, ({}, []))
            if dma is None:
                rr[0][eng] = oid
            else:
                rr[1].append(oid)
        for r in writes:
            self.res_w[r] = [oid]
            self.res_r[r] = ({}, [])
        return oid

    def flush(self):
        nc = self.nc
        ops = self.ops
        self.ops = []
        for (oid, eng, fn, deps, dma) in ops:
            inf = self.info[oid]
            if dma is not None:
                if dma not in self.dsem:
                    self.dsem[dma] = nc.alloc_semaphore(name=f"dsem_{dma}")
                    self.dcount[dma] = 0
                self.dcount[dma] += 16
                inf["ev"] = (self.dsem[dma], self.dcount[dma], "d_" + dma)
            elif inf["needs"]:
                self.ecount[eng] += 1
                inf["ev"] = (self.esem[eng], self.ecount[eng], "e_" + eng)
        per_eng = {e: [] for e in self.ENGS}
        for o in ops:
            per_eng[o[1]].append(o)

        def emit(e, engine):
            waited = self.waited[e]
            for (oid, eng, fn, deps, dma) in per_eng[e]:
                need = {}
                for d in deps:
                    di = self.info[d]
                    if self._skip(di, e):
                        continue
                    sem, val, key = di["ev"]
                    if waited.get(key, 0) >= val:
                        continue
                    if key not in need or need[key][1] < val:
                        need[key] = (sem, val)
                for key, (sem, val) in need.items():
                    engine.wait_ge(sem, val)
                    waited[key] = val
                ins = fn(engine)
                ev = self.info[oid]["ev"]
                if ev is not None:
                    ins.then_inc(ev[0], 16 if dma is not None else 1)

        with nc.Block() as block:
            @block.tensor
            def _(t):
                emit("pe", t)

            @block.scalar
            def _(s):
                emit("act", s)

            @block.vector
            def _(v):
                emit("dve", v)

            @block.gpsimd
            def _(g):
                emit("pool", g)

            @block.sync
            def _(s):
                emit("sp", s)

    def final_wait(self, eng, res_list):
        deps = set()
        for r in res_list:
            deps.update(self.res_w.get(r, ()))
        for d in deps:
            self.info[d]["needs"] = True
        oid = self.nops
        self.nops += 1
        self.info[oid] = {"eng": eng, "dma": None, "needs": False, "ev": None}
        self.ops.append((oid, eng, lambda e: e.nop(), sorted(deps), None))


class Ctx:
    def __init__(self):
        self.nc = bass.Bass("TRN2", target_bir_lowering=False)
        self.nc.dge_precook = False
        self.p = Prog(self.nc)
        self.ins = {}
        self.outs = {}
        self.uid = 0
        self.outres = []
        self.ps = None

    def din(self, name, shape, dt=F32R):
        t = self.nc.dram_tensor(name, list(shape), dt, kind="ExternalInput").ap()
        self.ins[name] = t
        return t

    def dout(self, name, shape):
        t = self.nc.dram_tensor(name, list(shape), F32, kind="ExternalOutput").ap()
        self.outs[name] = t
        return t

    def tag(self, s):
        self.uid += 1
        return f"{s}{self.uid}"


def cv(ap, c=128):
    return ap.rearrange("(c p) t -> p c t", p=c)


class Stage:
    def __init__(self, cx, name):
        self.cx = cx
        self.nc = cx.nc
        self.p = cx.p
        self.name = cx.tag(name)
        self.es = ExitStack()
        self.ps = [self.es.enter_context(self.nc.psum_tensor(f"{self.name}_ps{i}", [128, 512], F32)) for i in range(8)]
        self.cnt = {}

    def sb(self, name, shape, dt=F32):
        return self.es.enter_context(self.nc.sbuf_tensor(f"{self.name}_{name}", list(shape), dt))

    def R(self, *a):
        return (self.name,) + a

    def rot(self, key, n):
        v = self.cnt.get(key, 0)
        self.cnt[key] = v + 1
        return v % n

    def close(self):
        self.p.flush()
        self.es.close()

    def consts(self):
        p = self.p
        self.onesf = self.sb("onesf", [128, 128])
        self.ones = self.sb("ones", [128, 128], F32R)
        self.epsb = self.sb("epsb", [128, 1])
        p.op("pool", lambda e: e.memset(self.onesf[:], 1.0), writes=[self.R("onesf")])
        p.op("pool", lambda e: e.memset(self.epsb[:], EPS), writes=[self.R("epsb")])
        p.op("act", lambda e: e.activation(out=self.ones[:], in_=self.onesf[:], func=AF.Copy),
             reads=[self.R("onesf")], writes=[self.R("ones")])
        self.sq = [self.sb(f"sq{i}", [128, TT], F32R) for i in range(2)]
        self.rstd = self.sb("rstd", [128, TT])

    def load_small(self, name, src, shape):
        t = self.sb(name, shape)
        self.p.op("sp", lambda e: e.dma_start(out=t[:], in_=src), writes=[self.R(name)], dma="sm_" + name)
        return t

    def rms(self, srcs, nfeat, psi, n=TT):
        p = self.p
        ps = self.ps[psi]
        for i, (ap, rows, rk) in enumerate(srcs):
            b = self.rot("sq", 2)
            sq = self.sq[b]
            p.op("act", lambda e, ap=ap, rows=rows, sq=sq: e.activation(out=sq[0:rows, 0:n], in_=ap, func=AF.Square),
                 reads=list(rk), writes=[self.R("sq", b)])
            p.op("pe", lambda e, rows=rows, sq=sq, i=i: e.matmul(ps[:, 0:n], self.ones[0:rows, :], sq[0:rows, 0:n],
                                                              start=(i == 0), stop=(i == len(srcs) - 1)),
                 reads=[self.R("sq", b), self.R("ones")], writes=[("ps", psi)])
        p.op("act", lambda e: e.activation(out=self.rstd[:, 0:n], in_=ps[:, 0:n], func=AF.Sqrt, bias=self.epsb[:], scale=1.0 / nfeat),
             reads=[("ps", psi), self.R("epsb")], writes=[self.R("rstd")])
        p.op("dve", lambda e: e.reciprocal(out=self.rstd[:, 0:n], in_=self.rstd[:, 0:n]),
             reads=[self.R("rstd")], writes=[self.R("rstd")])

    def linear(self, W, K, cols, rhs_fn, evac, wb, psis, n=TT):
        p = self.p
        nk = (K + 127) // 128
        full = K // 128
        tail = K - full * 128
        for idx, pieces in enumerate(cols):
            b = self.rot(("w", id(wb)), len(wb))
            wt = wb[b]
            wres = []
            off = 0
            for pi, (c0, m) in enumerate(pieces):
                if full:
                    rk = self.R("w", id(wb), b, pi, "m")
                    p.op("sp", lambda e, c0=c0, m=m, off=off, wt=wt: e.dma_start(
                        out=wt[:, 0:full, off:off + m], in_=cv(W[0:full * 128, c0:c0 + m])),
                        writes=[rk], dma=f"w{len(wb[0].shape)}{wb[0].shape[1]}_{b}_{pi}m")
                    wres.append(rk)
                if tail:
                    rk = self.R("w", id(wb), b, pi, "t")
                    p.op("sp", lambda e, c0=c0, m=m, off=off, wt=wt: e.dma_start(
                        out=wt[0:tail, full, off:off + m], in_=W[full * 128:K, c0:c0 + m]),
                        writes=[rk], dma=f"w{len(wb[0].shape)}{wb[0].shape[1]}_{b}_{pi}t")
                    wres.append(rk)
                off += m
            M = off
            psi = psis[self.rot(("ps", tuple(psis)), len(psis))]
            ps = self.ps[psi]
            for c in range(nk):
                rows = min(128, K - c * 128)
                rhs, rk = rhs_fn(c)
                p.op("pe", lambda e, c=c, rows=rows, rhs=rhs, wt=wt, ps=ps: e.matmul(
                    ps[:, 0:n], wt[0:rows, c, :], rhs, start=(c == 0), stop=(c == nk - 1)),
                    reads=wres + list(rk), writes=[("ps", psi)])
            evac(idx, ps, M, psi)

    def store(self, dst, src_ap, reads, writes, key, q="act"):
        self.p.op(q, lambda e: e.dma_start(out=dst, in_=src_ap), reads=reads, writes=writes, dma=key)


def xres(t, i=None):
    if i is None:
        return [("x", t, i) for i in range(KC)]
    return [("x", t, i)]


def load_norm_h(st, ht, xsrc, gain, t):
    p = st.p
    tsl = slice(t * TT, (t + 1) * TT)
    p.op("sp", lambda e: e.dma_start(out=ht[:], in_=cv(xsrc.bitcast(F32R))[:, :, tsl]),
         reads=xres(t), writes=[st.R("ht")], dma="ht")
    st.rms([(ht[:, c, :].bitcast(F32), 128, [st.R("ht")]) for c in range(KC)], D, 7)
    for c in range(KC):
        p.op("dve", lambda e, c=c: e.scalar_tensor_tensor(out=ht[:, c, :], in0=ht[:, c, :].bitcast(F32), scalar=gain[:, c:c + 1],
                                                          in1=st.rstd[:], op0=ALU.mult, op1=ALU.mult),
             reads=[st.R("ht"), st.R("gain"), st.R("rstd")], writes=[st.R("ht")])


def outproj(st, W, nk, rhs_fn, wo, xsrc, xdst, t, scale, xr, xo):
    p = st.p
    tsl = slice(t * TT, (t + 1) * TT)
    H = wo[0].shape[1]
    nh = (nk + H - 1) // H
    Wv = cv(W)
    xs_v = cv(xsrc)
    xd_v = cv(xdst)
    for i in range(KC):
        yb = st.rot("y", 2)
        psi = 5 + yb
        py = st.ps[psi]
        for h in range(nh):
            b = st.rot("wo", len(wo))
            k0 = h * H
            k1 = min(nk, k0 + H)
            rk = st.R("wo", b)
            p.op("sp", lambda e, i=i, k0=k0, k1=k1, b=b: e.dma_start(out=wo[b][:, 0:k1 - k0, :], in_=Wv[:, k0:k1, i * 128:(i + 1) * 128]),
                 writes=[rk], dma=f"wo{b}")
            for j in range(k0, k1):
                rhs, rr = rhs_fn(j)
                p.op("pe", lambda e, j=j, k0=k0, b=b, rhs=rhs, py=py: e.matmul(py[:], wo[b][:, j - k0, :], rhs,
                                                                            start=(j == 0), stop=(j == nk - 1)),
                     reads=[rk] + list(rr), writes=[("ps", psi)])
        p.op("act", lambda e, i=i, yb=yb: e.dma_start(out=xr[yb][:], in_=xs_v[:, i, tsl]),
             reads=xres(t, i), writes=[st.R("xr", yb)], dma=f"xr{yb}")
        p.op("dve", lambda e, yb=yb, py=py: e.scalar_tensor_tensor(out=xo[yb][:], in0=py[:], scalar=float(scale), in1=xr[yb][:],
                                                                  op0=ALU.mult, op1=ALU.add),
             reads=[("ps", psi), st.R("xr", yb)], writes=[st.R("xo", yb)])
        p.op("act", lambda e, i=i, yb=yb: e.dma_start(out=xd_v[:, i, tsl], in_=xo[yb][:]),
             reads=[st.R("xo", yb)], writes=xres(t, i), dma=f"xo{yb}")


def ffn_stage(cx, xsrc, xdst, w_in, w_out, gain_src):
    st = Stage(cx, "ffn")
    p = st.p
    st.consts()
    ht = st.sb("ht", [128, KC, TT], F32R)
    act = st.sb("act", [128, FC, TT], F32R)
    wg = [st.sb(f"wg{i}", [128, KC, 128], F32R) for i in range(2)]
    wu = [st.sb(f"wu{i}", [128, KC, 128], F32R) for i in range(2)]
    wo = [st.sb(f"wo{i}", [128, FC // 2, 128], F32R) for i in range(3)]
    sg = [st.sb(f"sg{i}", [128, TT]) for i in range(2)]
    xr = [st.sb(f"xr{i}", [128, TT]) for i in range(2)]
    xo = [st.sb(f"xo{i}", [128, TT]) for i in range(2)]
    gain = st.load_small("gain", gain_src, [128, KC])
    w_in_v = cv(w_in)
    for t in range(NTOK // TT):
        load_norm_h(st, ht, xsrc if t >= 0 else xsrc, gain, t)
        for j in range(FC):
            b = st.rot("wi", 2)
            p.op("sp", lambda e, j=j, b=b: e.dma_start(out=wg[b][:], in_=w_in_v[:, :, j * 128:(j + 1) * 128]),
                 writes=[st.R("wg", b)], dma=f"wg{b}")
            p.op("sp", lambda e, j=j, b=b: e.dma_start(out=wu[b][:], in_=w_in_v[:, :, DFF + j * 128:DFF + (j + 1) * 128]),
                 writes=[st.R("wu", b)], dma=f"wu{b}")
            pg, pu = st.ps[1 + b], st.ps[3 + b]
            for c in range(KC):
                p.op("pe", lambda e, c=c, b=b, pg=pg: e.matmul(pg[:], wg[b][:, c, :], ht[:, c, :], start=(c == 0), stop=(c == KC - 1)),
                     reads=[st.R("wg", b), st.R("ht")], writes=[("ps", 1 + b)])
            for c in range(KC):
                p.op("pe", lambda e, c=c, b=b, pu=pu: e.matmul(pu[:], wu[b][:, c, :], ht[:, c, :], start=(c == 0), stop=(c == KC - 1)),
                     reads=[st.R("wu", b), st.R("ht")], writes=[("ps", 3 + b)])
            p.op("act", lambda e, b=b, pg=pg: e.activation(out=sg[b][:], in_=pg[:], func=AF.Silu),
                 reads=[("ps", 1 + b)], writes=[st.R("sg", b)])
            p.op("dve", lambda e, j=j, b=b, pu=pu: e.tensor_tensor(out=act[:, j, :], in0=sg[b][:], in1=pu[:], op=ALU.mult),
                 reads=[st.R("sg", b), ("ps", 3 + b)], writes=[st.R("act", j)])
        outproj(st, w_out, FC, lambda j: (act[:, j, :], [st.R("act", j)]), wo, xsrc, xdst, t, 0.5, xr, xo)
    st.close()


def mem_prep(st, memT, mem_gain_s, w_mem_kv, mem_k_gain_s, memt, mkT, mv, wb, wv):
    p = st.p
    p.op("sp", lambda e: e.dma_start(out=memt[:], in_=cv(memT)), writes=[st.R("memt")], dma="memt")
    st.rms([(memt[:, c, :].bitcast(F32), 128, [st.R("memt")]) for c in range(KC)], D, 7, n=256)
    for c in range(KC):
        p.op("dve", lambda e, c=c: e.scalar_tensor_tensor(out=memt[:, c, :], in0=memt[:, c, :].bitcast(F32), scalar=mem_gain_s[:, c:c + 1],
                                                          in1=st.rstd[:, 0:256], op0=ALU.mult, op1=ALU.mult),
             reads=[st.R("memt"), st.R("mem_gain"), st.R("rstd")], writes=[st.R("memt")])
    mks = st.sb("mks", [128, 256])

    def evac(idx, ps, M, psi):
        p.op("act", lambda e: e.activation(out=mks[:], in_=ps[:, 0:256], func=AF.Copy), reads=[("ps", psi)], writes=[st.R("mks")])
        st.rms([(mks[:], 128, [st.R("mks")])], 128, 7, n=256)
        p.op("dve", lambda e: e.scalar_tensor_tensor(out=mkT[idx][:], in0=mks[:], scalar=mem_k_gain_s[:, 0:1], in1=st.rstd[:, 0:256],
                                                     op0=ALU.mult, op1=ALU.mult),
             reads=[st.R("mks"), st.R("mem_k_gain"), st.R("rstd")], writes=[st.R("mkT", idx)])

    st.linear(w_mem_kv, D, [[(m * 128, 128)] for m in range(4)], lambda c: (memt[:, c, :], [st.R("memt")]), evac, wb, [1, 2], n=256)
    p.op("sp", lambda e: e.dma_start(out=wv[:], in_=cv(w_mem_kv)[:, :, 512:1024]), writes=[st.R("wv")], dma="wv")
    for j in range(2):
        ps = st.ps[3 + j]
        for c in range(KC):
            p.op("pe", lambda e, c=c, j=j, ps=ps: e.matmul(ps[:], memt[:, c, j * 128:(j + 1) * 128], wv[:, c, :], start=(c == 0), stop=(c == KC - 1)),
                 reads=[st.R("memt"), st.R("wv")], writes=[("ps", 3 + j)])
        p.op("act", lambda e, j=j, ps=ps: e.activation(out=mv[j][:], in_=ps[:], func=AF.Copy), reads=[("ps", 3 + j)], writes=[st.R("mv", j)])


def mem_attn(st, m, ps, psi, mkT, mv, mem_q_gain_s, memo_dst, t, bufs):
    p = st.p
    mqs, qn, pt, rden, mo = bufs
    tsl = slice(t * TT, (t + 1) * TT)
    p.op("act", lambda e: e.activation(out=mqs[:], in_=ps[:], func=AF.Copy), reads=[("ps", psi)], writes=[st.R("mqs")])
    st.rms([(mqs[:], 128, [st.R("mqs")])], 128, 7)
    p.op("dve", lambda e: e.scalar_tensor_tensor(out=qn[:], in0=mqs[:], scalar=mem_q_gain_s[:, 0:1], in1=st.rstd[:], op0=ALU.mult, op1=ALU.mult),
         reads=[st.R("mqs"), st.R("mem_q_gain"), st.R("rstd")], writes=[st.R("qn")])
    for j in range(2):
        p.op("pe", lambda e, j=j: e.matmul(st.ps[3 + j][:], mkT[m][:, j * 128:(j + 1) * 128], qn[:], start=True, stop=True),
             reads=[st.R("mkT", m), st.R("qn")], writes=[("ps", 3 + j)])
        p.op("act", lambda e, j=j: e.activation(out=pt[j][:], in_=st.ps[3 + j][:], func=AF.Exp, scale=128.0 ** -0.5),
             reads=[("ps", 3 + j)], writes=[st.R("pt", j)])
    for j in range(2):
        p.op("pe", lambda e, j=j: e.matmul(st.ps[5][:], mv[j][:, m * 128:(m + 1) * 128], pt[j][:], start=(j == 0), stop=(j == 1)),
             reads=[st.R("mv", j), st.R("pt", j)], writes=[("ps", 5)])
    for j in range(2):
        p.op("pe", lambda e, j=j: e.matmul(st.ps[6][:], st.ones[:], pt[j][:], start=(j == 0), stop=(j == 1)),
             reads=[st.R("ones"), st.R("pt", j)], writes=[("ps", 6)])
    p.op("dve", lambda e: e.reciprocal(out=rden[:], in_=st.ps[6][:]), reads=[("ps", 6)], writes=[st.R("rden")])
    p.op("dve", lambda e: e.tensor_tensor(out=mo[:], in0=st.ps[5][:], in1=rden[:], op=ALU.mult),
         reads=[("ps", 5), st.R("rden")], writes=[st.R("mo")])
    st.store(memo_dst[m * 128:(m + 1) * 128, tsl], mo[:], [st.R("mo")], [("memo", t, m)], "mo")


def mixin_stage(cx, kind, xsrc, W, gains, memT, w_mem_kv, outs, extra):
    st = Stage(cx, "mix" + kind)
    p = st.p
    st.consts()
    ht = st.sb("ht", [128, KC, TT], F32R)
    gain = st.load_small("gain", gains["mix_gain"], [128, KC])
    mem_gain_s = st.load_small("mem_gain", gains["mem_gain"], [128, KC])
    mem_q_gain_s = st.load_small("mem_q_gain", gains["mem_q_gain"], [128, 1])
    mem_k_gain_s = st.load_small("mem_k_gain", gains["mem_k_gain"], [128, 1])
    wb = [st.sb(f"wb{i}", [128, KC, 128], F32R) for i in range(3)]
    memt = st.sb("memt", [128, KC, 256], F32R)
    wv = st.sb("wv", [128, KC, 512], F32R)
    mkT = [st.sb(f"mkT{m}", [128, 256], F32R) for m in range(4)]
    mv = [st.sb(f"mv{j}", [128, 512], F32R) for j in range(2)]
    mem_prep(st, memT, mem_gain_s, w_mem_kv, mem_k_gain_s, memt, mkT, mv, wb, wv)
    bufs = (st.sb("mqs", [128, TT]), st.sb("qn", [128, TT], F32R), [st.sb(f"pt{j}", [128, TT], F32R) for j in range(2)],
            st.sb("rden", [128, TT]), st.sb("mo", [128, TT]))
    stg = [st.sb(f"stg{i}", [128, TT]) for i in range(3)]

    def stage_out(ps, psi, M, dst, wres, func=AF.Copy, bias=None):
        b = st.rot("stg", 3)
        eng = "act" if (bias is not None or st.rot("ev", 2) == 0) else "dve"
        if eng == "act":
            if bias is not None:
                p.op("act", lambda e: e.activation(out=stg[b][0:M, :], in_=ps[0:M, :], func=AF.Identity, bias=bias),
                     reads=[("ps", psi), st.R("bg")], writes=[st.R("stg", b)])
            else:
                p.op("act", lambda e: e.activation(out=stg[b][0:M, :], in_=ps[0:M, :], func=func), reads=[("ps", psi)], writes=[st.R("stg", b)])
        else:
            p.op("dve", lambda e: e.tensor_copy(out=stg[b][0:M, :], in_=ps[0:M, :]), reads=[("ps", psi)], writes=[st.R("stg", b)])
        st.store(dst, stg[b][0:M, :], [st.R("stg", b)], wres, f"stg{b}", q="pool")

    if kind == "A":
        bg = st.load_small("bg", extra["b_gates"], [8, 1])
        cols = [[(c * 128, 128)] for c in range(36)] + [[(4608, 8)]] + [[(4616 + m * 128, 128)] for m in range(4)]
        for t in range(NTOK // TT):
            tsl = slice(t * TT, (t + 1) * TT)
            load_norm_h(st, ht, xsrc, gain, t)

            def evac(idx, ps, M, psi, t=t, tsl=tsl):
                if idx < 12:
                    stage_out(ps, psi, 128, outs["qkT"][idx * 128:(idx + 1) * 128, tsl], [("qkT", t, idx)])
                elif idx < 24:
                    i = idx - 12
                    stage_out(ps, psi, 128, outs["vT"][i * 128:(i + 1) * 128, tsl], [("vT", t, i)])
                elif idx < 36:
                    i = idx - 24
                    stage_out(ps, psi, 128, outs["oT"][i * 128:(i + 1) * 128, tsl], [("oT", t, i)])
                elif idx == 36:
                    stage_out(ps, psi, 8, outs["gT"][:, tsl], [("gT", t)], bias=bg[0:8, 0:1])
                else:
                    mem_attn(st, idx - 37, ps, psi, mkT, mv, mem_q_gain_s, outs["memoT"], t, bufs)

            st.linear(W, D, cols, lambda c: (ht[:, c, :], [st.R("ht")]), evac, wb, [1, 2])
    else:
        qlg = st.load_small("qlg", gains["q_latent_gain"], [128, 4])
        qg = st.load_small("qg", gains["q_gain"], [128, 3])
        cosF = st.sb("cosF", [64, NTOK])
        sinS = st.sb("sinS", [64, NTOK])
        p.op("sp", lambda e: e.dma_start(out=cosF[:], in_=extra["cosF"]), writes=[st.R("cosF")], dma="cosF")
        p.op("sp", lambda e: e.dma_start(out=sinS[:], in_=extra["sinS"]), writes=[st.R("sinS")], dma="sinS")
        cqs = st.sb("cqs", [128, 4, TT])
        cqn = st.sb("cqn", [128, 4, TT], F32R)
        wq = [st.sb(f"wq{i}", [128, 4, 128], F32R) for i in range(3)]
        qhn = st.sb("qhn", [128, TT])
        qhr = st.sb("qhr", [64, TT])
        qhs = st.sb("qhs", [64, TT])
        ra = st.sb("ra", [64, TT])
        rb = st.sb("rb", [64, TT])
        W_uq = extra["w_uq"]
        cols = [[(0, 128)], [(128, 128)], [(256, 128)], [(384, 64)]] + [[(448 + m * 128, 128)] for m in range(4)]
        for t in range(NTOK // TT):
            tsl = slice(t * TT, (t + 1) * TT)
            load_norm_h(st, ht, xsrc, gain, t)

            def evac(idx, ps, M, psi, t=t, tsl=tsl):
                if idx < 4:
                    p.op("act", lambda e: e.activation(out=cqs[0:M, idx, :], in_=ps[0:M, :], func=AF.Copy),
                         reads=[("ps", psi)], writes=[st.R("cqs", idx)])
                else:
                    mem_attn(st, idx - 4, ps, psi, mkT, mv, mem_q_gain_s, outs["memoT"], t, bufs)

            st.linear(W, D, cols, lambda c: (ht[:, c, :], [st.R("ht")]), evac, wb, [1, 2])
            rows = [128, 128, 128, 64]
            st.rms([(cqs[0:rows[c], c, :], rows[c], [st.R("cqs", c)]) for c in range(4)], 448, 7)
            for c in range(4):
                p.op("dve", lambda e, c=c: e.scalar_tensor_tensor(out=cqn[0:rows[c], c, :], in0=cqs[0:rows[c], c, :], scalar=qlg[0:rows[c], c:c + 1],
                                                                  in1=st.rstd[0:rows[c], :], op0=ALU.mult, op1=ALU.mult),
                     reads=[st.R("cqs", c), st.R("qlg"), st.R("rstd")], writes=[st.R("cqn", c)])
            for hd in range(12):
                b0 = hd * 192
                hcols = [[(b0, 128)], [(b0 + 128, 64)], [(b0 + 160, 32), (b0 + 128, 32)]]

                def evac2(idx, ps, M, psi, hd=hd, t=t, tsl=tsl):
                    dstb = [qhn, qhr, qhs][idx]
                    p.op("act", lambda e: e.activation(out=dstb[0:M, :], in_=ps[0:M, :], func=AF.Copy),
                         reads=[("ps", psi)], writes=[st.R("qh", idx)])

                st.linear(W_uq, 448, hcols, lambda c: (cqn[0:rows[c], c, :], [st.R("cqn", c)]), evac2, wq, [1, 2])
                st.rms([(qhn[:], 128, [st.R("qh", 0)]), (qhr[:], 64, [st.R("qh", 1)])], 192, 7)
                b = st.rot("stg", 3)
                p.op("dve", lambda e, b=b: e.scalar_tensor_tensor(out=stg[b][:], in0=qhn[:], scalar=qg[:, 0:1], in1=st.rstd[:], op0=ALU.mult, op1=ALU.mult),
                     reads=[st.R("qh", 0), st.R("qg"), st.R("rstd")], writes=[st.R("stg", b)])
                st.store(outs["QT"][hd, 0:128, tsl], stg[b][:], [st.R("stg", b)], [("QT", t, hd, 0)], f"stg{b}", q="pool")
                p.op("dve", lambda e: e.scalar_tensor_tensor(out=ra[:], in0=qhr[:], scalar=qg[0:64, 1:2], in1=st.rstd[0:64, :], op0=ALU.mult, op1=ALU.mult),
                     reads=[st.R("qh", 1), st.R("qg"), st.R("rstd")], writes=[st.R("ra")])
                p.op("dve", lambda e: e.scalar_tensor_tensor(out=rb[:], in0=qhs[:], scalar=qg[0:64, 2:3], in1=st.rstd[0:64, :], op0=ALU.mult, op1=ALU.mult),
                     reads=[st.R("qh", 2), st.R("qg"), st.R("rstd")], writes=[st.R("rb")])
                p.op("pool", lambda e, tsl=tsl: e.tensor_tensor(out=ra[:], in0=ra[:], in1=cosF[:, tsl], op=ALU.mult),
                     reads=[st.R("ra"), st.R("cosF")], writes=[st.R("ra")])
                p.op("pool", lambda e, tsl=tsl: e.tensor_tensor(out=rb[:], in0=rb[:], in1=sinS[:, tsl], op=ALU.mult),
                     reads=[st.R("rb"), st.R("sinS")], writes=[st.R("rb")])
                b = st.rot("stg", 3)
                p.op("dve", lambda e, b=b: e.tensor_tensor(out=stg[b][0:64, :], in0=ra[:], in1=rb[:], op=ALU.add),
                     reads=[st.R("ra"), st.R("rb")], writes=[st.R("stg", b)])
                st.store(outs["QT"][hd, 128:192, tsl], stg[b][0:64, :], [st.R("stg", b)], [("QT", t, hd, 1)], f"stg{b}", q="pool")
    st.close()


def sharedkv_stage(cx, xsrc, gains, w_dkv, w_ukv, extra, outs):
    st = Stage(cx, "skv")
    p = st.p
    st.consts()
    ht = st.sb("ht", [128, KC, TT], F32R)
    gain = st.load_small("gain", gains["kv_gain"], [128, KC])
    klg = st.load_small("klg", gains["kv_latent_gain"], [128, 4])
    kg = st.load_small("kg", gains["k_gain"], [128, 3])
    cosF = st.sb("cosF", [64, NTOK])
    sinS = st.sb("sinS", [64, NTOK])
    p.op("sp", lambda e: e.dma_start(out=cosF[:], in_=extra["cosF"]), writes=[st.R("cosF")], dma="cosF")
    p.op("sp", lambda e: e.dma_start(out=sinS[:], in_=extra["sinS"]), writes=[st.R("sinS")], dma="sinS")
    wb = [st.sb(f"wb{i}", [128, KC, 128], F32R) for i in range(3)]
    wq = [st.sb(f"wq{i}", [128, 4, 128], F32R) for i in range(3)]
    cks = st.sb("cks", [128, 4, TT])
    ckn = st.sb("ckn", [128, 4, TT], F32R)
    kpr = st.sb("kpr", [64, TT])
    kps = st.sb("kps", [64, TT])
    sqpe = st.sb("sqpe", [64, TT], F32R)
    Rk = st.sb("Rk", [64, TT])
    rb = st.sb("rb", [64, TT])
    kn = st.sb("kn", [128, TT])
    sqn = st.sb("sqn", [128, TT], F32R)
    stg = [st.sb(f"stg{i}", [128, TT]) for i in range(3)]
    cols = [[(c * 128, 128)] for c in range(4)] + [[(512, 64)], [(544, 32), (512, 32)]]
    for t in range(NTOK // TT):
        tsl = slice(t * TT, (t + 1) * TT)
        load_norm_h(st, ht, xsrc, gain, t)

        def evac(idx, ps, M, psi):
            if idx < 4:
                p.op("act", lambda e: e.activation(out=cks[:, idx, :], in_=ps[:], func=AF.Copy), reads=[("ps", psi)], writes=[st.R("cks", idx)])
            else:
                dstb = kpr if idx == 4 else kps
                p.op("act", lambda e: e.activation(out=dstb[:], in_=ps[0:64, :], func=AF.Copy), reads=[("ps", psi)], writes=[st.R("kp", idx)])

        st.linear(w_dkv, D, cols, lambda c: (ht[:, c, :], [st.R("ht")]), evac, wb, [1, 2])
        st.rms([(cks[:, c, :], 128, [st.R("cks", c)]) for c in range(4)], 512, 7)
        for c in range(4):
            p.op("dve", lambda e, c=c: e.scalar_tensor_tensor(out=ckn[:, c, :], in0=cks[:, c, :], scalar=klg[:, c:c + 1], in1=st.rstd[:],
                                                              op0=ALU.mult, op1=ALU.mult),
                 reads=[st.R("cks", c), st.R("klg"), st.R("rstd")], writes=[st.R("ckn", c)])
        p.op("act", lambda e: e.activation(out=sqpe[:], in_=kpr[:], func=AF.Square), reads=[st.R("kp", 4)], writes=[st.R("sqpe")])
        p.op("dve", lambda e, tsl=tsl: e.scalar_tensor_tensor(out=Rk[:], in0=kpr[:], scalar=kg[0:64, 1:2], in1=cosF[:, tsl], op0=ALU.mult, op1=ALU.mult),
             reads=[st.R("kp", 4), st.R("kg"), st.R("cosF")], writes=[st.R("Rk")])
        p.op("dve", lambda e, tsl=tsl: e.scalar_tensor_tensor(out=rb[:], in0=kps[:], scalar=kg[0:64, 2:3], in1=sinS[:, tsl], op0=ALU.mult, op1=ALU.mult),
             reads=[st.R("kp", 5), st.R("kg"), st.R("sinS")], writes=[st.R("rb")])
        p.op("dve", lambda e: e.tensor_tensor(out=Rk[:], in0=Rk[:], in1=rb[:], op=ALU.add), reads=[st.R("Rk"), st.R("rb")], writes=[st.R("Rk")])
        for hd in range(12):
            hcols = [[(hd * 256, 128)], [(hd * 256 + 128, 128)]]

            def evac2(idx, ps, M, psi, hd=hd, t=t, tsl=tsl):
                if idx == 0:
                    p.op("act", lambda e: e.activation(out=kn[:], in_=ps[:], func=AF.Copy), reads=[("ps", psi)], writes=[st.R("kn")])
                else:
                    b = st.rot("stg", 3)
                    p.op("dve", lambda e, b=b: e.tensor_copy(out=stg[b][:], in_=ps[:]), reads=[("ps", psi)], writes=[st.R("stg", b)])
                    st.store(outs["VT"][hd, :, tsl], stg[b][:], [st.R("stg", b)], [("VT", t, hd)], f"stg{b}", q="pool")

            st.linear(w_ukv, 512, hcols, lambda c: (ckn[:, c, :], [st.R("ckn", c)]), evac2, wq, [1, 2])
            p.op("act", lambda e: e.activation(out=sqn[:], in_=kn[:], func=AF.Square), reads=[st.R("kn")], writes=[st.R("sqn")])
            p.op("pe", lambda e: e.matmul(st.ps[7][:], st.ones[:], sqn[:], start=True, stop=False),
                 reads=[st.R("ones"), st.R("sqn")], writes=[("ps", 7)])
            p.op("pe", lambda e: e.matmul(st.ps[7][:], st.ones[0:64, :], sqpe[:], start=False, stop=True),
                 reads=[st.R("ones"), st.R("sqpe")], writes=[("ps", 7)])
            p.op("act", lambda e: e.activation(out=st.rstd[:], in_=st.ps[7][:], func=AF.Sqrt, bias=st.epsb[:], scale=1.0 / 192),
                 reads=[("ps", 7), st.R("epsb")], writes=[st.R("rstd")])
            p.op("dve", lambda e: e.reciprocal(out=st.rstd[:], in_=st.rstd[:]), reads=[st.R("rstd")], writes=[st.R("rstd")])
            b = st.rot("stg", 3)
            p.op("dve", lambda e, b=b: e.scalar_tensor_tensor(out=stg[b][:], in0=kn[:], scalar=kg[:, 0:1], in1=st.rstd[:], op0=ALU.mult, op1=ALU.mult),
                 reads=[st.R("kn"), st.R("kg"), st.R("rstd")], writes=[st.R("stg", b)])
            st.store(outs["KT"][hd, 0:128, tsl], stg[b][:], [st.R("stg", b)], [("KT", t, hd, 0)], f"stg{b}", q="pool")
            b = st.rot("stg", 3)
            p.op("dve", lambda e, b=b: e.tensor_tensor(out=stg[b][0:64, :], in0=Rk[:], in1=st.rstd[0:64, :], op=ALU.mult),
                 reads=[st.R("Rk"), st.R("rstd")], writes=[st.R("stg", b)])
            st.store(outs["KT"][hd, 128:192, tsl], stg[b][0:64, :], [st.R("stg", b)], [("KT", t, hd, 1)], f"stg{b}", q="pool")
    st.close()


def mixout_stage(cx, xsrc, xdst, mainT, memoT, oT, w_out):
    st = Stage(cx, "mo")
    p = st.p
    ht = st.sb("ht", [128, KC, TT], F32R)
    og = st.sb("og", [128, 12, TT]) if oT is not None else None
    wo = [st.sb(f"wo{i}", [128, KC, 128], F32R) for i in range(3)]
    xr = [st.sb(f"xr{i}", [128, TT]) for i in range(2)]
    xo = [st.sb(f"xo{i}", [128, TT]) for i in range(2)]
    for t in range(NTOK // TT):
        tsl = slice(t * TT, (t + 1) * TT)
        p.op("sp", lambda e, tsl=tsl: e.dma_start(out=ht[:, 0:12, :], in_=cv(mainT)[:, :, tsl]), writes=[st.R("htm")], dma="htm")
        p.op("sp", lambda e, tsl=tsl: e.dma_start(out=ht[:, 12:16, :], in_=cv(memoT)[:, :, tsl]), writes=[st.R("hte")], dma="hte")
        if oT is not None:
            p.op("sp", lambda e, tsl=tsl: e.dma_start(out=og[:], in_=cv(oT)[:, :, tsl]), writes=[st.R("og")], dma="og")
            p.op("act", lambda e: e.activation(out=og[:], in_=og[:], func=AF.Sigmoid), reads=[st.R("og")], writes=[st.R("og")])
            for c in range(12):
                p.op("dve", lambda e, c=c: e.tensor_tensor(out=ht[:, c, :], in0=ht[:, c, :].bitcast(F32), in1=og[:, c, :], op=ALU.mult),
                     reads=[st.R("htm"), st.R("og")], writes=[st.R("htm")])
        outproj(st, w_out, KC, lambda j: (ht[:, j, :], [st.R("htm") if j < 12 else st.R("hte")]), wo, xsrc, xdst, t, 1.0, xr, xo)
    st.close()


def seqmix_program(mode):
    cx = Ctx()
    nc, p = cx.nc, cx.p
    NH = 3 if mode == "mla" else 1
    DV = 128 if mode == "mla" else 384
    NV = DV // 128
    masks_d = cx.din("masks", [4, 128, TT], F32)
    if mode == "mla":
        QT = cx.din("QT", [NH, 192, SEQ])
        KT = cx.din("KT", [NH, 192, SEQ])
        V = cx.din("V", [NH, SEQ, DV])
        OT = cx.dout("OT", [NH, DV, SEQ])
    else:
        QP = cx.din("QP", [192, SEQ + 3], F32)
        KP = cx.din("KP", [192, SEQ + 3], F32)
        V = cx.din("V", [1, SEQ, DV])
        IG = cx.din("IG", [128, 64], F32)
        FG = cx.din("FG", [128, 64], F32)
        CW = cx.din("CW", [128, 4, 4], F32)
        HG = cx.din("HG", [128, 3], F32)
        IDN = cx.din("IDN", [128, 128], F32)
        LTRI = cx.din("LTRI", [128, 128], F32)
        USTR = cx.din("USTR", [64, 64], F32)
        OT = cx.dout("OT", [1, DV, SEQ])
    st = Stage(cx, "sm")
    st.consts()
    masks = st.sb("masks", [128, 4, TT])
    p.op("sp", lambda e: e.dma_start(out=masks[:], in_=masks_d.rearrange("j p t -> p j t")), writes=[st.R("masks")], dma="masks")
    kT_hi = st.sb("kT_hi", [128, SEQ], F32R)
    kT_lo = st.sb("kT_lo", [64, SEQ], F32R)
    q_hi = [st.sb(f"q_hi{i}", [128, TT], F32R) for i in range(2)]
    q_lo = [st.sb(f"q_lo{i}", [64, TT], F32R) for i in range(2)]
    vt = [st.sb(f"vt{i}", [128, DV], F32R) for i in range(3)]
    wT = [st.sb(f"wT{i}", [128, TT], F32R) for i in range(2)]
    et = [st.sb(f"et{i}", [128, TT]) for i in range(2)]
    rden = st.sb("rden", [128, TT])
    ho = [st.sb(f"ho{i}", [128, TT]) for i in range(NV)]
    NUM0 = 2
    if mode == "mlstm":
        HS = SEQ // 2
        tmp = st.sb("tmp", [128, HS + 3])
        acc = st.sb("acc", [128, HS])
        cw = st.load_small("cw", CW, [128, 4, 4])
        hg = st.load_small("hg", HG, [128, 3])
        idn = st.load_small("idn", IDN, [128, 128])
        ltri = st.load_small("ltri", LTRI, [128, 128])
        ustr = st.load_small("ustr", USTR, [64, 64])
        ig = st.load_small("ig", IG, [128, 64])
        fg = st.load_small("fg", FG, [128, 64])
        Bbc = st.sb("Bbc", [128, SEQ])
        bias = st.sb("bias", [128, 64])
        lf = st.sb("lf", [128, 64])
        bb = st.sb("bb", [128, 64])
        totbc = st.sb("totbc", [64, 128])
        dg = [st.sb(f"dg{i}", [128, 128]) for i in range(2)]

        def conv(src_ap, rows, n, widx, dst_ap, scale, rk_src, rk_dst):
            a = acc[0:rows, 0:n]
            p.op("dve", lambda e: e.tensor_scalar(out=a, in0=src_ap[:, 3:3 + n], scalar1=cw[0:rows, widx, 3:4], scalar2=None, op0=ALU.mult),
                 reads=[rk_src, st.R("cw")], writes=[st.R("acc")])
            for j in range(3):
                p.op("dve", lambda e, j=j: e.scalar_tensor_tensor(out=a, in0=src_ap[:, j:j + n], scalar=cw[0:rows, widx, j:j + 1], in1=a,
                                                                  op0=ALU.mult, op1=ALU.add),
                     reads=[rk_src, st.R("cw"), st.R("acc")], writes=[st.R("acc")])
            p.op("act", lambda e: e.activation(out=a, in_=a, func=AF.Silu), reads=[st.R("acc")], writes=[st.R("acc")])
            p.op("act", lambda e: e.activation(out=dst_ap, in_=a, func=AF.Copy, scale=float(scale)), reads=[st.R("acc")], writes=[rk_dst])

        for (r0, rows, dstt, widx) in ((0, 128, kT_hi, 2), (128, 64, kT_lo, 3)):
            for hh in range(2):
                p.op("sp", lambda e, r0=r0, rows=rows, hh=hh: e.dma_start(out=tmp[0:rows, :], in_=KP[r0:r0 + rows, hh * HS:(hh + 1) * HS + 3]),
                     writes=[st.R("tmp")], dma="tmp")
                conv(tmp[0:rows, :], rows, HS, widx, dstt[:, hh * HS:(hh + 1) * HS], 1.0, st.R("tmp"), st.R("kT", r0, hh))
        p.op("act", lambda e: e.activation(out=lf[:], in_=fg[:], func=AF.Exp, scale=-1.0), reads=[st.R("fg")], writes=[st.R("lf")])
        p.op("act", lambda e: e.activation(out=lf[:], in_=lf[:], func=AF.Ln, bias=st.onesf[:, 0:1], scale=1.0),
             reads=[st.R("lf"), st.R("onesf")], writes=[st.R("lf")])
        p.op("dve", lambda e: e.tensor_scalar(out=lf[:], in0=lf[:], scalar1=-1.0, scalar2=None, op0=ALU.mult), reads=[st.R("lf")], writes=[st.R("lf")])
        p.op("pe", lambda e: e.matmul(st.ps[6][0:64, 0:128], lf[:], st.onesf[:], start=True, stop=True),
             reads=[st.R("lf"), st.R("onesf")], writes=[("ps", 6)])
        p.op("dve", lambda e: e.tensor_copy(out=totbc[:], in_=st.ps[6][0:64, 0:128]), reads=[("ps", 6)], writes=[st.R("totbc")])
        p.op("pe", lambda e: e.matmul(st.ps[7][:, 0:64], ltri[:], lf[:], start=True, stop=False),
             reads=[st.R("ltri"), st.R("lf")], writes=[("ps", 7)])
        p.op("pe", lambda e: e.matmul(st.ps[7][:, 0:64], totbc[:], ustr[:], start=False, stop=True),
             reads=[st.R("totbc"), st.R("ustr")], writes=[("ps", 7)])
        p.op("dve", lambda e: e.tensor_copy(out=bb[:], in_=st.ps[7][:, 0:64]), reads=[("ps", 7)], writes=[st.R("bb")])
        p.op("dve", lambda e: e.tensor_tensor(out=bias[:], in0=ig[:], in1=bb[:], op=ALU.subtract), reads=[st.R("ig"), st.R("bb")], writes=[st.R("bias")])
        for g in range(16):
            for k in range(4):
                blk = g * 4 + k
                d = st.rot("dg", 2)
                p.op("dve", lambda e, blk=blk, d=d: e.tensor_scalar(out=dg[d][:], in0=idn[:], scalar1=bb[:, blk:blk + 1], scalar2=None, op0=ALU.mult),
                     reads=[st.R("idn"), st.R("bb")], writes=[st.R("dg", d)])
                p.op("pe", lambda e, k=k, d=d: e.matmul(st.ps[6][:, k * 128:(k + 1) * 128], st.onesf[:], dg[d][:], start=True, stop=True),
                     reads=[st.R("onesf"), st.R("dg", d)], writes=[("ps", 6)])
            p.op("act", lambda e, g=g: e.activation(out=Bbc[:, g * TT:(g + 1) * TT], in_=st.ps[6][:], func=AF.Copy),
                 reads=[("ps", 6)], writes=[st.R("Bbc", g)])

    for hd in range(NH):
        if mode == "mla":
            p.op("sp", lambda e, hd=hd: e.dma_start(out=kT_hi[:], in_=KT[hd, 0:128, :]), writes=[st.R("kT", 0)], dma="kthi")
            p.op("sp", lambda e, hd=hd: e.dma_start(out=kT_lo[:], in_=KT[hd, 128:192, :]), writes=[st.R("kT", 128)], dma="ktlo")
        for qt in range(SEQ // TT):
            qsl = slice(qt * TT, (qt + 1) * TT)
            qb = st.rot("q", 2)
            if mode == "mla":
                p.op("sp", lambda e, hd=hd, qsl=qsl, qb=qb: e.dma_start(out=q_hi[qb][:], in_=QT[hd, 0:128, qsl]), writes=[st.R("qh", qb)], dma=f"qh{qb}")
                p.op("sp", lambda e, hd=hd, qsl=qsl, qb=qb: e.dma_start(out=q_lo[qb][:], in_=QT[hd, 128:192, qsl]), writes=[st.R("ql", qb)], dma=f"ql{qb}")
            else:
                for (r0, rows, dstt, widx, rkn) in ((0, 128, q_hi[qb], 0, "qh"), (128, 64, q_lo[qb], 1, "ql")):
                    p.op("sp", lambda e, r0=r0, rows=rows, qt=qt: e.dma_start(out=tmp[0:rows, 0:TT + 3], in_=QP[r0:r0 + rows, qt * TT:(qt + 1) * TT + 3]),
                         writes=[st.R("tmp")], dma="tmp")
                    conv(tmp[0:rows, 0:TT + 3], rows, TT, widx, dstt[:], 192.0 ** -0.5, st.R("tmp"), st.R(rkn, qb))
            nst = 4 * (qt + 1)
            for s in range(nst):
                ssl = slice(s * 128, (s + 1) * 128)
                sb_ = st.rot("S", 2)
                psS = st.ps[sb_]
                p.op("pe", lambda e, ssl=ssl, qb=qb, psS=psS: e.matmul(psS[:], kT_hi[:, ssl], q_hi[qb][:], start=True, stop=False),
                     reads=[st.R("kT", 0), st.R("kT", 0, 0), st.R("kT", 0, 1), st.R("qh", qb)], writes=[("ps", sb_)])
                p.op("pe", lambda e, ssl=ssl, qb=qb, psS=psS: e.matmul(psS[:], kT_lo[:, ssl], q_lo[qb][:], start=False, stop=True),
                     reads=[st.R("kT", 128), st.R("kT", 128, 0), st.R("kT", 128, 1), st.R("ql", qb)], writes=[("ps", sb_)])
                vb = st.rot("v", 3)
                p.op("act", lambda e, hd=hd, ssl=ssl, vb=vb: e.dma_start(out=vt[vb][:], in_=V[hd, ssl, :]), writes=[st.R("vt", vb)], dma=f"vt{vb}")
                wbi = st.rot("wT", 2)
                dj = s - 4 * qt
                if mode == "mla":
                    if dj >= 0:
                        p.op("dve", lambda e, wbi=wbi, dj=dj, psS=psS: e.scalar_tensor_tensor(out=et[wbi][:], in0=psS[:], scalar=192.0 ** -0.5,
                                                                                         in1=masks[:, dj, :], op0=ALU.mult, op1=ALU.add),
                             reads=[("ps", sb_), st.R("masks")], writes=[st.R("et", wbi)])
                        p.op("act", lambda e, wbi=wbi: e.activation(out=wT[wbi][:], in_=et[wbi][:], func=AF.Exp),
                             reads=[st.R("et", wbi)], writes=[st.R("wT", wbi)])
                    else:
                        p.op("act", lambda e, wbi=wbi, psS=psS: e.activation(out=wT[wbi][:], in_=psS[:], func=AF.Exp, scale=192.0 ** -0.5),
                             reads=[("ps", sb_)], writes=[st.R("wT", wbi)])
                else:
                    if dj >= 0:
                        p.op("pool", lambda e, wbi=wbi, dj=dj, qsl=qsl: e.tensor_tensor(out=et[wbi][:], in0=Bbc[:, qsl], in1=masks[:, dj, :], op=ALU.add),
                             reads=[st.R("Bbc", qt), st.R("masks")], writes=[st.R("et", wbi)])
                        p.op("act", lambda e, wbi=wbi, s=s: e.activation(out=et[wbi][:], in_=et[wbi][:], func=AF.Exp, bias=bias[:, s:s + 1], scale=1.0),
                             reads=[st.R("et", wbi), st.R("bias")], writes=[st.R("et", wbi)])
                    else:
                        p.op("act", lambda e, wbi=wbi, s=s, qsl=qsl: e.activation(out=et[wbi][:], in_=Bbc[:, qsl], func=AF.Exp, bias=bias[:, s:s + 1], scale=1.0),
                             reads=[st.R("Bbc", qt), st.R("bias")], writes=[st.R("et", wbi)])
                    p.op("dve", lambda e, wbi=wbi, psS=psS: e.tensor_tensor(out=wT[wbi][:], in0=psS[:], in1=et[wbi][:], op=ALU.mult),
                         reads=[("ps", sb_), st.R("et", wbi)], writes=[st.R("wT", wbi)])
                for c in range(NV):
                    p.op("pe", lambda e, c=c, vb=vb, wbi=wbi, s=s: e.matmul(st.ps[NUM0 + c][:], vt[vb][:, c * 128:(c + 1) * 128], wT[wbi][:],
                                                                           start=(s == 0), stop=(s == nst - 1)),
                         reads=[st.R("vt", vb), st.R("wT", wbi)], writes=[("ps", NUM0 + c)])
                p.op("pe", lambda e, wbi=wbi, s=s: e.matmul(st.ps[5][:], st.ones[:], wT[wbi][:], start=(s == 0), stop=(s == nst - 1)),
                     reads=[st.R("ones"), st.R("wT", wbi)], writes=[("ps", 5)])
            if mode == "mla":
                p.op("dve", lambda e: e.reciprocal(out=rden[:], in_=st.ps[5][:]), reads=[("ps", 5)], writes=[st.R("rden")])
            else:
                p.op("act", lambda e: e.activation(out=rden[:], in_=st.ps[5][:], func=AF.Abs), reads=[("ps", 5)], writes=[st.R("rden")])
                p.op("dve", lambda e: e.tensor_scalar(out=rden[:], in0=rden[:], scalar1=1.0, scalar2=None, op0=ALU.max),
                     reads=[st.R("rden")], writes=[st.R("rden")])
                p.op("dve", lambda e: e.reciprocal(out=rden[:], in_=rden[:]), reads=[st.R("rden")], writes=[st.R("rden")])
            for c in range(NV):
                p.op("dve", lambda e, c=c: e.tensor_tensor(out=ho[c][:], in0=st.ps[NUM0 + c][:], in1=rden[:], op=ALU.mult),
                     reads=[("ps", NUM0 + c), st.R("rden")], writes=[st.R("ho", c)])
            if mode == "mlstm":
                st.rms([(ho[c][:], 128, [st.R("ho", c)]) for c in range(NV)], DV, 7)
                for c in range(NV):
                    p.op("dve", lambda e, c=c: e.scalar_tensor_tensor(out=ho[c][:], in0=ho[c][:], scalar=hg[:, c:c + 1], in1=st.rstd[:],
                                                                      op0=ALU.mult, op1=ALU.mult),
                         reads=[st.R("ho", c), st.R("hg"), st.R("rstd")], writes=[st.R("ho", c)])
            for c in range(NV):
                p.op("pool", lambda e, hd=hd, c=c, qsl=qsl: e.dma_start(out=OT[hd, c * 128:(c + 1) * 128, qsl], in_=ho[c][:]),
                     reads=[st.R("ho", c)], writes=[("OT", hd, c, qt)], dma=f"ot{c}")
                cx.outres.append(("OT", hd, c, qt))
    p.final_wait("sp", cx.outres)
    st.close()
    return cx


def gains_in(cx, names):
    g = {}
    for n, shp in names.items():
        g[n] = cx.din("g_" + n, shp, F32)
    return g


def tprog(idx):
    cx = Ctx()
    xin = cx.din("xin", [D, NTOK], F32)
    xs = cx.dout("xs", [D, NTOK])
    first = [True]

    def src():
        s = xin if first[0] else xs
        first[0] = False
        return s

    def ffn(tag):
        w_in = cx.din(tag + "_w_in", [D, 2 * DFF])
        w_out = cx.din(tag + "_w_out", [DFF, D])
        g = cx.din(tag + "_gain", [128, KC], F32)
        ffn_stage(cx, src(), xs, w_in, w_out, g)

    def mixout(kindA):
        mainT = cx.din("mainT", [1536, NTOK])
        memoTi = cx.din("memoT_in", [512, NTOK])
        oT = cx.din("oT_in", [1536, NTOK], F32) if kindA else None
        w_out = cx.din("w_out", [D, D])
        mixout_stage(cx, src(), xs, mainT, memoTi, oT, w_out)

    def mixin(kind):
        memT = cx.din("memT", [D, 256])
        w_mem_kv = cx.din("w_mem_kv", [D, 1024])
        outs = {"memoT": cx.dout("memoT", [512, NTOK])}
        names = {"mix_gain": [128, KC], "mem_gain": [128, KC], "mem_q_gain": [128, 1], "mem_k_gain": [128, 1]}
        extra = {}
        if kind == "A":
            W = cx.din("a_w_in", [D, 5128])
            extra["b_gates"] = cx.din("b_gates", [8, 1], F32)
            outs["qkT"] = cx.dout("qkT", [1536, NTOK])
            outs["vT"] = cx.dout("vT", [1536, NTOK])
            outs["oT"] = cx.dout("oT", [1536, NTOK])
            outs["gT"] = cx.dout("gT", [8, NTOK])
        else:
            W = cx.din("b_w_in", [D, 960])
            extra["w_uq"] = cx.din("b_w_uq", [448, 2304])
            extra["cosF"] = cx.din("cosF", [64, NTOK], F32)
            extra["sinS"] = cx.din("sinS", [64, NTOK], F32)
            names["q_latent_gain"] = [128, 4]
            names["q_gain"] = [128, 3]
            outs["QT"] = cx.dout("QT", [12, 192, NTOK])
        g = gains_in(cx, names)
        mixin_stage(cx, kind, xs if not first[0] else src(), W, g, memT, w_mem_kv, outs, extra)

    def skv():
        g = gains_in(cx, {"kv_gain": [128, KC], "kv_latent_gain": [128, 4], "k_gain": [128, 3]})
        w_dkv = cx.din("w_dkv", [D, 576])
        w_ukv = cx.din("w_ukv", [512, 3072])
        extra = {"cosF": cx.din("kcosF", [64, NTOK], F32), "sinS": cx.din("ksinS", [64, NTOK], F32)}
        outs = {"KT": cx.dout("KT", [12, 192, NTOK]), "VT": cx.dout("VT", [12, 128, NTOK])}
        sharedkv_stage(cx, xs, g, w_dkv, w_ukv, extra, outs)

    if idx == 0:
        ffn("fa"); mixin("A")
    elif idx == 1:
        mixout(True); ffn("fa"); ffn("fb"); mixin("A")
    elif idx == 2:
        mixout(True); ffn("fa"); skv(); ffn("fb"); mixin("B")
    elif idx == 3:
        mixout(False); ffn("fa"); ffn("fb"); mixin("B")
    else:
        mixout(False); ffn("fa")
    allres = set()
    for r, w in cx.p.res_w.items():
        if isinstance(r, tuple) and r[0] in ("x", "memo", "qkT", "vT", "oT", "gT", "QT", "KT", "VT"):
            allres.add(r)
    cx.p.final_wait("sp", sorted(allres, key=str))
    cx.p.flush()
    return cx


def _run(cx, in_maps):
    names = set(cx.ins.keys())
    maps = [{k: np.ascontiguousarray(v, dtype=np.float32) for k, v in m.items() if k in names} for m in in_maps]
    for m in maps:
        assert set(m.keys()) == names, (sorted(names - set(m.keys())), sorted(set(m.keys()) - names))
    res = run_bass_kernel_spmd(cx.nc, maps, core_ids=list(range(NCORES)))
    return res.results


def colT(v, n=128):
    v = np.asarray(v, np.float32)
    return np.ascontiguousarray(v.reshape(-1, n).T)


def pad_cols(v, rows=128):
    out = np.zeros((rows, len(v)), np.float32)
    for i, a in enumerate(v):
        out[:len(a), i] = a
    return out


def kernel(x, mem, positions, ffn1_gain, ffn1_w_in, ffn1_w_out, mix_gain, w_out, mem_gain, w_mem_kv,
           mem_q_gain, mem_k_gain, a_w_in, a_b_gates, a_conv, a_head_gain, kv_gain, w_dkv, kv_latent_gain,
           w_ukv, k_gain, b_w_in, b_q_latent_gain, b_w_uq, b_q_gain, ffn2_gain, ffn2_w_in, ffn2_w_out,
           _debug=None):
    A = lambda a: np.asarray(a)
    x, mem, positions = A(x).astype(np.float32), A(mem).astype(np.float32), A(positions)
    W = {k: A(v) for k, v in dict(ffn1_gain=ffn1_gain, ffn1_w_in=ffn1_w_in, ffn1_w_out=ffn1_w_out, mix_gain=mix_gain, w_out=w_out,
                                   mem_gain=mem_gain, w_mem_kv=w_mem_kv, mem_q_gain=mem_q_gain, mem_k_gain=mem_k_gain, a_w_in=a_w_in,
                                   a_b_gates=a_b_gates, a_conv=a_conv, a_head_gain=a_head_gain, kv_gain=kv_gain, w_dkv=w_dkv,
                                   kv_latent_gain=kv_latent_gain, w_ukv=w_ukv, k_gain=k_gain, b_w_in=b_w_in,
                                   b_q_latent_gain=b_q_latent_gain, b_w_uq=b_w_uq, b_q_gain=b_q_gain, ffn2_gain=ffn2_gain,
                                   ffn2_w_in=ffn2_w_in, ffn2_w_out=ffn2_w_out).items()}
    cores = range(NCORES)
    bq = [(c // 4, c % 4) for c in cores]
    tok = lambda c: slice(bq[c][1] * NTOK, (bq[c][1] + 1) * NTOK)
    xT = [np.ascontiguousarray(x[bq[c][0], tok(c)].T) for c in cores]
    memT = [np.ascontiguousarray(mem[bq[c][0]].T) for c in cores]
    inv_freq = (10000.0 ** (-np.arange(0, 64, 2, dtype=np.float32) / np.float32(64))).astype(np.float32)
    cosF, sinS = [], []
    for c in cores:
        ang = positions[bq[c][0], tok(c)].astype(np.float32)[None, :] * inv_freq[:, None]
        cs, sn = np.cos(ang).astype(np.float32), np.sin(ang).astype(np.float32)
        cosF.append(np.concatenate([cs, cs], 0))
        sinS.append(np.concatenate([-sn, sn], 0))
    masks = np.zeros((4, 128, TT), np.float32)
    for j in range(4):
        pp = np.arange(128)[:, None]
        tt = np.arange(TT)[None, :]
        masks[j] = np.where(tt >= pp + 128 * j, 0.0, MASKV)
    dbg = {}

    def ffn_in(m, tag, which, L):
        m[tag + "_w_in"] = W[which + "_w_in"][L]
        m[tag + "_w_out"] = W[which + "_w_out"][L]
        m[tag + "_gain"] = colT(W[which + "_gain"][L])

    def mix_in(m, c, L, kind):
        m["memT"] = memT[c]
        m["w_mem_kv"] = W["w_mem_kv"][L]
        m["g_mix_gain"] = colT(W["mix_gain"][L])
        m["g_mem_gain"] = colT(W["mem_gain"][L])
        m["g_mem_q_gain"] = W["mem_q_gain"][L].reshape(128, 1)
        m["g_mem_k_gain"] = W["mem_k_gain"][L].reshape(128, 1)
        if kind == "A":
            m["a_w_in"] = W["a_w_in"][L]
            m["b_gates"] = W["a_b_gates"][L].reshape(8, 1)
        else:
            j = L - 2
            m["b_w_in"] = W["b_w_in"][j]
            m["b_w_uq"] = W["b_w_uq"][j]
            m["cosF"], m["sinS"] = cosF[c], sinS[c]
            g = W["b_q_latent_gain"][j]
            m["g_q_latent_gain"] = pad_cols([g[0:128], g[128:256], g[256:384], g[384:448]])
            qg = W["b_q_gain"][j]
            m["g_q_gain"] = pad_cols([qg[0:128], qg[128:192], np.concatenate([qg[160:192], qg[128:160]])])

    def seq_full(parts, b):
        return np.concatenate([parts[b * 4 + q] for q in range(4)], axis=1)

    progs = {}

    def getprog(key, fn):
        if key not in progs:
            progs[key] = fn()
        return progs[key]

    maps = []
    for c in cores:
        m = {"xin": xT[c]}
        ffn_in(m, "fa", "ffn1", 0)
        mix_in(m, c, 0, "A")
        maps.append(m)
    if _debug is not None and "T0" in _debug:
        r = _debug["T0"]
    else:
        r = _run(tprog(0), maps)
    if _debug is not None:
        _debug["T0"] = r
        if _debug.get("stop") == "T0":
            return None
    xcur = [r[c]["xs"] for c in cores]

    def mlstm_layer(r, L):
        qkT = [seq_full([r[c]["qkT"] for c in cores], b) for b in range(2)]
        vT = [seq_full([r[c]["vT"] for c in cores], b) for b in range(2)]
        gT = [seq_full([r[c]["gT"] for c in cores], b) for b in range(2)]
        cw = W["a_conv"][L]
        hgain = W["a_head_gain"][L]
        maps = []
        for c in cores:
            b, h = c // 4, c % 4
            qp = np.zeros((192, SEQ + 3), np.float32)
            kp = np.zeros((192, SEQ + 3), np.float32)
            qp[:, 3:] = qkT[b][h * 192:(h + 1) * 192]
            kp[:, 3:] = qkT[b][768 + h * 192:768 + (h + 1) * 192]
            cwt = np.zeros((128, 4, 4), np.float32)
            cq = cw[:, h * 192:(h + 1) * 192].T
            ck = cw[:, 768 + h * 192:768 + (h + 1) * 192].T
            cwt[:, 0], cwt[:64, 1], cwt[:, 2], cwt[:64, 3] = cq[:128], cq[128:], ck[:128], ck[128:]
            m = {"masks": masks, "QP": qp, "KP": kp,
                 "V": np.ascontiguousarray(vT[b][h * 384:(h + 1) * 384].T)[None],
                 "IG": np.ascontiguousarray(gT[b][h].reshape(64, 128).T),
                 "FG": np.ascontiguousarray(gT[b][4 + h].reshape(64, 128).T),
                 "CW": cwt, "HG": colT(hgain[h * 384:(h + 1) * 384]),
                 "IDN": np.eye(128, dtype=np.float32), "LTRI": np.triu(np.ones((128, 128), np.float32)),
                 "USTR": np.triu(np.ones((64, 64), np.float32), 1)}
            maps.append(m)
        rr = _run(getprog("mlstm", lambda: seqmix_program("mlstm")), maps)
        main = []
        for c in cores:
            b, q = bq[c]
            main.append(np.concatenate([rr[b * 4 + h]["OT"][0][:, q * NTOK:(q + 1) * NTOK] for h in range(4)], axis=0))
        return main

    def mla_layer(r, KTf, Vf):
        QTf = [np.concatenate([r[b * 4 + q]["QT"] for q in range(4)], axis=2) for b in range(2)]
        maps = []
        for c in cores:
            b, g = c // 4, c % 4
            hs = slice(g * 3, g * 3 + 3)
            maps.append({"masks": masks, "QT": QTf[b][hs], "KT": KTf[b][hs], "V": Vf[b][hs]})
        rr = _run(getprog("mla", lambda: seqmix_program("mla")), maps)
        main = []
        for c in cores:
            b, q = bq[c]
            main.append(np.concatenate([rr[b * 4 + hd // 3]["OT"][hd % 3][:, q * NTOK:(q + 1) * NTOK] for hd in range(12)], axis=0))
        return main

    if _debug is not None and "main0" in _debug:
        main = _debug["main0"]
    else:
        main = mlstm_layer(r, 0)
    if _debug is not None:
        _debug["main0"] = main
        if _debug.get("stop") == "H0":
            return None
    maps = []
    for c in cores:
        m = {"xin": xcur[c], "mainT": main[c], "memoT_in": r[c]["memoT"], "oT_in": r[c]["oT"], "w_out": W["w_out"][0]}
        ffn_in(m, "fa", "ffn2", 0)
        ffn_in(m, "fb", "ffn1", 1)
        mix_in(m, c, 1, "A")
        maps.append(m)
    if _debug is not None and "T1" in _debug:
        r = _debug["T1"]
    else:
        r = _run(getprog("T1", lambda: tprog(1)), maps)
    xcur = [r[c]["xs"] for c in cores]
    if _debug is not None:
        _debug["T1"] = r
        if _debug.get("stop") == "T1":
            return None
    main = mlstm_layer(r, 1)
    maps = []
    for c in cores:
        m = {"xin": xcur[c], "mainT": main[c], "memoT_in": r[c]["memoT"], "oT_in": r[c]["oT"], "w_out": W["w_out"][1]}
        ffn_in(m, "fa", "ffn2", 1)
        ffn_in(m, "fb", "ffn1", 2)
        mix_in(m, c, 2, "B")
        m["g_kv_gain"] = colT(W["kv_gain"])
        m["g_kv_latent_gain"] = colT(W["kv_latent_gain"])
        kg = W["k_gain"]
        m["g_k_gain"] = pad_cols([kg[0:128], kg[128:192], np.concatenate([kg[160:192], kg[128:160]])])
        m["w_dkv"], m["w_ukv"] = W["w_dkv"], W["w_ukv"]
        m["kcosF"], m["ksinS"] = cosF[c], sinS[c]
        maps.append(m)
    r = _run(tprog(2), maps)
    xcur = [r[c]["xs"] for c in cores]
    if _debug is not None:
        _debug["T2"] = r
        if _debug.get("stop") == "T2":
            return None
    KTf = [np.concatenate([r[b * 4 + q]["KT"] for q in range(4)], axis=2) for b in range(2)]
    Vf = [np.ascontiguousarray(np.concatenate([r[b * 4 + q]["VT"] for q in range(4)], axis=2).transpose(0, 2, 1)) for b in range(2)]
    main = mla_layer(r, KTf, Vf)
    maps = []
    for c in cores:
        m = {"xin": xcur[c], "mainT": main[c], "memoT_in": r[c]["memoT"], "w_out": W["w_out"][2]}
        ffn_in(m, "fa", "ffn2", 2)
        ffn_in(m, "fb", "ffn1", 3)
        mix_in(m, c, 3, "B")
        maps.append(m)
    r = _run(tprog(3), maps)
    xcur = [r[c]["xs"] for c in cores]
    main = mla_layer(r, KTf, Vf)
    maps = []
    for c in cores:
        m = {"xin": xcur[c], "mainT": main[c], "memoT_in": r[c]["memoT"], "w_out": W["w_out"][3]}
        ffn_in(m, "fa", "ffn2", 3)
        maps.append(m)
    r = _run(tprog(4), maps)
    out = np.zeros((2, SEQ, D), np.float32)
    for c in cores:
        out[bq[c][0], tok(c)] = r[c]["xs"].T
    return out
```

```python
import numpy as np
from contextlib import ExitStack
import concourse.bass as bass
import concourse.mybir as mybir
from concourse.bass_utils import run_bass_kernel_spmd

F32 = mybir.dt.float32
F32R = mybir.dt.float32r
I32 = mybir.dt.int32
AF = mybir.ActivationFunctionType
ALU = mybir.AluOpType

D = 2048
DFF = 5632
KC = 16
FC = 44
TT = 512
EPS = 1e-6
SEQ = 8192
NTOK = 2048
NCORES = 8
MASKV = -30000.0

SAME_ENGINE_SYNC = True


class Prog:
    ENGS = ("pe", "act", "dve", "pool", "sp")

    def __init__(self, nc):
        self.nc = nc
        self.ops = []
        self.nops = 0
        self.res_w = {}
        self.res_r = {}
        self.info = {}
        self.esem = {e: nc.alloc_semaphore(name=f"sem_{e}") for e in self.ENGS}
        self.ecount = {e: 0 for e in self.ENGS}
        self.dsem = {}
        self.dcount = {}
        self.waited = {e: {} for e in self.ENGS}

    def _skip(self, di, eng):
        return di["dma"] is None and di["eng"] == eng and (eng == "pe" or not SAME_ENGINE_SYNC)

    def op(self, eng, fn, reads=(), writes=(), dma=None):
        deps = set()
        for r in reads:
            deps.update(self.res_w.get(r, ()))
        for r in writes:
            deps.update(self.res_w.get(r, ()))
            rr = self.res_r.get(r)
            if rr:
                deps.update(rr[0].values())
                deps.update(rr[1])
        oid = self.nops
        self.nops += 1
        self.info[oid] = {"eng": eng, "dma": dma, "needs": False, "ev": None}
        for d in deps:
            di = self.info[d]
            if not self._skip(di, eng):
                di["needs"] = True
        self.ops.append((oid, eng, fn, sorted(deps), dma))
        for r in reads:
            rr = self.res_r.setdefault(r, ({}, []))
            if dma is None:
                rr[0][eng] = oid
            else:
                rr[1].append(oid)
        for r in writes:
            self.res_w[r] = [oid]
            self.res_r[r] = ({}, [])
        return oid

    def flush(self):
        nc = self.nc
        ops = self.ops
        self.ops = []
        for (oid, eng, fn, deps, dma) in ops:
            inf = self.info[oid]
            if dma is not None:
                if dma not in self.dsem:
                    self.dsem[dma] = nc.alloc_semaphore(name=f"dsem_{dma}")
                    self.dcount[dma] = 0
                self.dcount[dma] += 16
                inf["ev"] = (self.dsem[dma], self.dcount[dma], "d_" + dma)
            elif inf["needs"]:
                self.ecount[eng] += 1
                inf["ev"] = (self.esem[eng], self.ecount[eng], "e_" + eng)
        per_eng = {e: [] for e in self.ENGS}
        for o in ops:
            per_eng[o[1]].append(o)

        def emit(e, engine):
            waited = self.waited[e]
            for (oid, eng, fn, deps, dma) in per_eng[e]:
                need = {}
                for d in deps:
                    di = self.info[d]
                    if self._skip(di, e):
                        continue
                    sem, val, key = di["ev"]
                    if waited.get(key, 0) >= val:
                        continue
                    if key not in need or need[key][1] < val:
                        need[key] = (sem, val)
                for key, (sem, val) in need.items():
                    engine.wait_ge(sem, val)
                    waited[key] = val
                ins = fn(engine)
                ev = self.info[oid]["ev"]
                if ev is not None:
                    ins.then_inc(ev[0], 16 if dma is not None else 1)

        with nc.Block() as block:
            @block.tensor
            def _(t):
                emit("pe", t)

            @block.scalar
            def _(s):
                emit("act", s)

            @block.vector
            def _(v):
                emit("dve", v)

            @block.gpsimd
            def _(g):
                emit("pool", g)

            @block.sync
            def _(s):
                emit("sp", s)

    def final_wait(self, eng, res_list):
        deps = set()
        for r in res_list:
            deps.update(self.res_w.get(r, ()))
        for d in deps:
            self.info[d]["needs"] = True
        oid = self.nops
        self.nops += 1
        self.info[oid] = {"eng": eng, "dma": None, "needs": False, "ev": None}
        self.ops.append((oid, eng, lambda e: e.nop(), sorted(deps), None))


class Ctx:
    def __init__(self):
        self.nc = bass.Bass("TRN2", target_bir_lowering=False)
        self.nc.dge_precook = False
        self.p = Prog(self.nc)
        self.ins = {}
        self.outs = {}
        self.uid = 0
        self.outres = []
        self.ps = None

    def din(self, name, shape, dt=F32R):
        t = self.nc.dram_tensor(name, list(shape), dt, kind="ExternalInput").ap()
        self.ins[name] = t
        return t

    def dout(self, name, shape):
        t = self.nc.dram_tensor(name, list(shape), F32, kind="ExternalOutput").ap()
        self.outs[name] = t
        return t

    def tag(self, s):
        self.uid += 1
        return f"{s}{self.uid}"


def cv(ap, c=128):
    return ap.rearrange("(c p) t -> p c t", p=c)


class Stage:
    def __init__(self, cx, name):
        self.cx = cx
        self.nc = cx.nc
        self.p = cx.p
        self.name = cx.tag(name)
        self.es = ExitStack()
        self.ps = [self.es.enter_context(self.nc.psum_tensor(f"{self.name}_ps{i}", [128, 512], F32)) for i in range(8)]
        self.cnt = {}

    def sb(self, name, shape, dt=F32):
        return self.es.enter_context(self.nc.sbuf_tensor(f"{self.name}_{name}", list(shape), dt))

    def R(self, *a):
        return (self.name,) + a

    def rot(self, key, n):
        v = self.cnt.get(key, 0)
        self.cnt[key] = v + 1
        return v % n

    def close(self):
        self.p.flush()
        self.es.close()

    def consts(self):
        p = self.p
        self.onesf = self.sb("onesf", [128, 128])
        self.ones = self.sb("ones", [128, 128], F32R)
        self.epsb = self.sb("epsb", [128, 1])
        p.op("pool", lambda e: e.memset(self.onesf[:], 1.0), writes=[self.R("onesf")])
        p.op("pool", lambda e: e.memset(self.epsb[:], EPS), writes=[self.R("epsb")])
        p.op("act", lambda e: e.activation(out=self.ones[:], in_=self.onesf[:], func=AF.Copy),
             reads=[self.R("onesf")], writes=[self.R("ones")])
        self.sq = [self.sb(f"sq{i}", [128, TT], F32R) for i in range(2)]
        self.rstd = self.sb("rstd", [128, TT])

    def load_small(self, name, src, shape):
        t = self.sb(name, shape)
        self.p.op("sp", lambda e: e.dma_start(out=t[:], in_=src), writes=[self.R(name)], dma="sm_" + name)
        return t

    def rms(self, srcs, nfeat, psi, n=TT):
        p = self.p
        ps = self.ps[psi]
        for i, (ap, rows, rk) in enumerate(srcs):
            b = self.rot("sq", 2)
            sq = self.sq[b]
            p.op("act", lambda e, ap=ap, rows=rows, sq=sq: e.activation(out=sq[0:rows, 0:n], in_=ap, func=AF.Square),
                 reads=list(rk), writes=[self.R("sq", b)])
            p.op("pe", lambda e, rows=rows, sq=sq, i=i: e.matmul(ps[:, 0:n], self.ones[0:rows, :], sq[0:rows, 0:n],
                                                              start=(i == 0), stop=(i == len(srcs) - 1)),
                 reads=[self.R("sq", b), self.R("ones")], writes=[("ps", psi)])
        p.op("act", lambda e: e.activation(out=self.rstd[:, 0:n], in_=ps[:, 0:n], func=AF.Sqrt, bias=self.epsb[:], scale=1.0 / nfeat),
             reads=[("ps", psi), self.R("epsb")], writes=[self.R("rstd")])
        p.op("dve", lambda e: e.reciprocal(out=self.rstd[:, 0:n], in_=self.rstd[:, 0:n]),
             reads=[self.R("rstd")], writes=[self.R("rstd")])

    def linear(self, W, K, cols, rhs_fn, evac, wb, psis, n=TT):
        p = self.p
        nk = (K + 127) // 128
        full = K // 128
        tail = K - full * 128
        for idx, pieces in enumerate(cols):
            b = self.rot(("w", id(wb)), len(wb))
            wt = wb[b]
            wres = []
            off = 0
            for pi, (c0, m) in enumerate(pieces):
                if full:
                    rk = self.R("w", id(wb), b, pi, "m")
                    p.op("sp", lambda e, c0=c0, m=m, off=off, wt=wt: e.dma_start(
                        out=wt[:, 0:full, off:off + m], in_=cv(W[0:full * 128, c0:c0 + m])),
                        writes=[rk], dma=f"w{len(wb[0].shape)}{wb[0].shape[1]}_{b}_{pi}m")
                    wres.append(rk)
                if tail:
                    rk = self.R("w", id(wb), b, pi, "t")
                    p.op("sp", lambda e, c0=c0, m=m, off=off, wt=wt: e.dma_start(
                        out=wt[0:tail, full, off:off + m], in_=W[full * 128:K, c0:c0 + m]),
                        writes=[rk], dma=f"w{len(wb[0].shape)}{wb[0].shape[1]}_{b}_{pi}t")
                    wres.append(rk)
                off += m
            M = off
            psi = psis[self.rot(("ps", tuple(psis)), len(psis))]
            ps = self.ps[psi]
            for c in range(nk):
                rows = min(128, K - c * 128)
                rhs, rk = rhs_fn(c)
                p.op("pe", lambda e, c=c, rows=rows, rhs=rhs, wt=wt, ps=ps: e.matmul(
                    ps[:, 0:n], wt[0:rows, c, :], rhs, start=(c == 0), stop=(c == nk - 1)),
                    reads=wres + list(rk), writes=[("ps", psi)])
            evac(idx, ps, M, psi)

    def store(self, dst, src_ap, reads, writes, key, q="act"):
        self.p.op(q, lambda e: e.dma_start(out=dst, in_=src_ap), reads=reads, writes=writes, dma=key)


PI = float(np.pi)
TWO_PI = float(2 * np.pi)
CW1 = 6.28125
CW2 = float(2 * np.pi - 6.28125)


def rope_tables(st, posb_src, invf_src, cosF, sinS):
    p = st.p
    invf = st.load_small("invf", invf_src, [64, 2])
    posi = st.sb("posi", [64, TT], I32)
    ang = st.sb("ang", [64, TT])
    kf = st.sb("kf", [64, TT])
    ki = st.sb("ki", [64, TT], I32)
    for t in range(NTOK // TT):
        tsl = slice(t * TT, (t + 1) * TT)
        p.op("sp", lambda e, tsl=tsl: e.dma_start(out=posi[:], in_=posb_src[:, tsl]), writes=[st.R("posi")], dma="posi")
        p.op("dve", lambda e: e.tensor_copy(out=ang[:], in_=posi[:]), reads=[st.R("posi")], writes=[st.R("ang")])
        p.op("dve", lambda e: e.tensor_scalar(out=ang[:], in0=ang[:], scalar1=invf[:, 0:1], scalar2=None, op0=ALU.mult),
             reads=[st.R("ang"), st.R("invf")], writes=[st.R("ang")])
        for (dst, shift, nm) in ((sinS, 0.0, "sinS"), (cosF, PI / 2, "cosF")):
            d = dst[:, tsl]
            rk = st.R(nm)
            p.op("dve", lambda e, shift=shift: e.tensor_scalar(out=kf[:], in0=ang[:], scalar1=shift, scalar2=1.0 / TWO_PI, op0=ALU.add, op1=ALU.mult),
                 reads=[st.R("ang")], writes=[st.R("kf")])
            p.op("dve", lambda e: e.tensor_copy(out=ki[:], in_=kf[:]), reads=[st.R("kf")], writes=[st.R("ki")])
            p.op("dve", lambda e: e.tensor_copy(out=kf[:], in_=ki[:]), reads=[st.R("ki")], writes=[st.R("kf")])
            p.op("dve", lambda e, d=d: e.scalar_tensor_tensor(out=d, in0=kf[:], scalar=-CW1, in1=ang[:], op0=ALU.mult, op1=ALU.add),
                 reads=[st.R("kf"), st.R("ang")], writes=[rk])
            p.op("dve", lambda e, d=d: e.scalar_tensor_tensor(out=d, in0=kf[:], scalar=-CW2, in1=d, op0=ALU.mult, op1=ALU.add),
                 reads=[st.R("kf"), rk], writes=[rk])
            if shift != 0.0:
                p.op("dve", lambda e, d=d, shift=shift: e.tensor_scalar(out=d, in0=d, scalar1=shift, scalar2=None, op0=ALU.add), reads=[rk], writes=[rk])
            p.op("dve", lambda e, d=d: e.tensor_scalar(out=kf[:], in0=d, scalar1=PI, scalar2=None, op0=ALU.is_gt), reads=[rk], writes=[st.R("kf")])
            p.op("dve", lambda e, d=d: e.scalar_tensor_tensor(out=d, in0=kf[:], scalar=-TWO_PI, in1=d, op0=ALU.mult, op1=ALU.add),
                 reads=[st.R("kf"), rk], writes=[rk])
            p.op("dve", lambda e, d=d: e.tensor_scalar(out=kf[:], in0=d, scalar1=-PI, scalar2=None, op0=ALU.is_lt), reads=[rk], writes=[st.R("kf")])
            p.op("dve", lambda e, d=d: e.scalar_tensor_tensor(out=d, in0=kf[:], scalar=TWO_PI, in1=d, op0=ALU.mult, op1=ALU.add),
                 reads=[st.R("kf"), rk], writes=[rk])
            p.op("act", lambda e, d=d: e.activation(out=d, in_=d, func=AF.Sin), reads=[rk], writes=[rk])
        p.op("dve", lambda e, tsl=tsl: e.tensor_scalar(out=sinS[:, tsl], in0=sinS[:, tsl], scalar1=invf[:, 1:2], scalar2=None, op0=ALU.mult),
             reads=[st.R("sinS"), st.R("invf")], writes=[st.R("sinS")])


def xres(t, i=None):
    if i is None:
        return [("x", t, i) for i in range(KC)]
    return [("x", t, i)]


def load_norm_h(st, ht, xsrc, gain, t):
    p = st.p
    tsl = slice(t * TT, (t + 1) * TT)
    p.op("sp", lambda e: e.dma_start(out=ht[:], in_=cv(xsrc.bitcast(F32R))[:, :, tsl]),
         reads=xres(t), writes=[st.R("ht")], dma="ht")
    st.rms([(ht[:, c, :].bitcast(F32), 128, [st.R("ht")]) for c in range(KC)], D, 7)
    for c in range(KC):
        p.op("dve", lambda e, c=c: e.scalar_tensor_tensor(out=ht[:, c, :], in0=ht[:, c, :].bitcast(F32), scalar=gain[:, c:c + 1],
                                                          in1=st.rstd[:], op0=ALU.mult, op1=ALU.mult),
             reads=[st.R("ht"), st.R("gain"), st.R("rstd")], writes=[st.R("ht")])


def outproj(st, W, nk, rhs_fn, wo, xsrc, xdst, t, scale, xr, xo):
    p = st.p
    tsl = slice(t * TT, (t + 1) * TT)
    H = wo[0].shape[1]
    nh = (nk + H - 1) // H
    Wv = cv(W)
    xs_v = cv(xsrc)
    xd_v = cv(xdst)
    for i in range(KC):
        yb = st.rot("y", 2)
        psi = 5 + yb
        py = st.ps[psi]
        for h in range(nh):
            b = st.rot("wo", len(wo))
            k0 = h * H
            k1 = min(nk, k0 + H)
            rk = st.R("wo", b)
            p.op("sp", lambda e, i=i, k0=k0, k1=k1, b=b: e.dma_start(out=wo[b][:, 0:k1 - k0, :], in_=Wv[:, k0:k1, i * 128:(i + 1) * 128]),
                 writes=[rk], dma=f"wo{b}")
            for j in range(k0, k1):
                rhs, rr = rhs_fn(j)
                p.op("pe", lambda e, j=j, k0=k0, b=b, rhs=rhs, py=py: e.matmul(py[:], wo[b][:, j - k0, :], rhs,
                                                                            start=(j == 0), stop=(j == nk - 1)),
                     reads=[rk] + list(rr), writes=[("ps", psi)])
        p.op("act", lambda e, i=i, yb=yb: e.dma_start(out=xr[yb][:], in_=xs_v[:, i, tsl]),
             reads=xres(t, i), writes=[st.R("xr", yb)], dma=f"xr{yb}")
        p.op("dve", lambda e, yb=yb, py=py: e.scalar_tensor_tensor(out=xo[yb][:], in0=py[:], scalar=float(scale), in1=xr[yb][:],
                                                                  op0=ALU.mult, op1=ALU.add),
             reads=[("ps", psi), st.R("xr", yb)], writes=[st.R("xo", yb)])
        p.op("act", lambda e, i=i, yb=yb: e.dma_start(out=xd_v[:, i, tsl], in_=xo[yb][:]),
             reads=[st.R("xo", yb)], writes=xres(t, i), dma=f"xo{yb}")


def ffn_stage(cx, xsrc, xdst, w_in, w_out, gain_src):
    st = Stage(cx, "ffn")
    p = st.p
    st.consts()
    ht = st.sb("ht", [128, KC, TT], F32R)
    act = st.sb("act", [128, FC, TT], F32R)
    wg = [st.sb(f"wg{i}", [128, KC, 128], F32R) for i in range(2)]
    wu = [st.sb(f"wu{i}", [128, KC, 128], F32R) for i in range(2)]
    wo = [st.sb(f"wo{i}", [128, FC // 2, 128], F32R) for i in range(3)]
    sg = [st.sb(f"sg{i}", [128, TT]) for i in range(2)]
    xr = [st.sb(f"xr{i}", [128, TT]) for i in range(2)]
    xo = [st.sb(f"xo{i}", [128, TT]) for i in range(2)]
    gain = st.load_small("gain", gain_src, [128, KC])
    w_in_v = cv(w_in)
    for t in range(NTOK // TT):
        load_norm_h(st, ht, xsrc if t >= 0 else xsrc, gain, t)
        for j in range(FC):
            b = st.rot("wi", 2)
            p.op("sp", lambda e, j=j, b=b: e.dma_start(out=wg[b][:], in_=w_in_v[:, :, j * 128:(j + 1) * 128]),
                 writes=[st.R("wg", b)], dma=f"wg{b}")
            p.op("sp", lambda e, j=j, b=b: e.dma_start(out=wu[b][:], in_=w_in_v[:, :, DFF + j * 128:DFF + (j + 1) * 128]),
                 writes=[st.R("wu", b)], dma=f"wu{b}")
            pg, pu = st.ps[1 + b], st.ps[3 + b]
            for c in range(KC):
                p.op("pe", lambda e, c=c, b=b, pg=pg: e.matmul(pg[:], wg[b][:, c, :], ht[:, c, :], start=(c == 0), stop=(c == KC - 1)),
                     reads=[st.R("wg", b), st.R("ht")], writes=[("ps", 1 + b)])
            for c in range(KC):
                p.op("pe", lambda e, c=c, b=b, pu=pu: e.matmul(pu[:], wu[b][:, c, :], ht[:, c, :], start=(c == 0), stop=(c == KC - 1)),
                     reads=[st.R("wu", b), st.R("ht")], writes=[("ps", 3 + b)])
            p.op("act", lambda e, b=b, pg=pg: e.activation(out=sg[b][:], in_=pg[:], func=AF.Silu),
                 reads=[("ps", 1 + b)], writes=[st.R("sg", b)])
            p.op("dve", lambda e, j=j, b=b, pu=pu: e.tensor_tensor(out=act[:, j, :], in0=sg[b][:], in1=pu[:], op=ALU.mult),
                 reads=[st.R("sg", b), ("ps", 3 + b)], writes=[st.R("act", j)])
        outproj(st, w_out, FC, lambda j: (act[:, j, :], [st.R("act", j)]), wo, xsrc, xdst, t, 0.5, xr, xo)
    st.close()


def mem_prep(st, memT, mem_gain_s, w_mem_kv, mem_k_gain_s, memt, mkT, mv, wb, wv):
    p = st.p
    p.op("sp", lambda e: e.dma_start(out=memt[:], in_=cv(memT)), writes=[st.R("memt")], dma="memt")
    st.rms([(memt[:, c, :].bitcast(F32), 128, [st.R("memt")]) for c in range(KC)], D, 7, n=256)
    for c in range(KC):
        p.op("dve", lambda e, c=c: e.scalar_tensor_tensor(out=memt[:, c, :], in0=memt[:, c, :].bitcast(F32), scalar=mem_gain_s[:, c:c + 1],
                                                          in1=st.rstd[:, 0:256], op0=ALU.mult, op1=ALU.mult),
             reads=[st.R("memt"), st.R("mem_gain"), st.R("rstd")], writes=[st.R("memt")])
    mks = st.sb("mks", [128, 256])

    def evac(idx, ps, M, psi):
        p.op("act", lambda e: e.activation(out=mks[:], in_=ps[:, 0:256], func=AF.Copy), reads=[("ps", psi)], writes=[st.R("mks")])
        st.rms([(mks[:], 128, [st.R("mks")])], 128, 7, n=256)
        p.op("dve", lambda e: e.scalar_tensor_tensor(out=mkT[idx][:], in0=mks[:], scalar=mem_k_gain_s[:, 0:1], in1=st.rstd[:, 0:256],
                                                     op0=ALU.mult, op1=ALU.mult),
             reads=[st.R("mks"), st.R("mem_k_gain"), st.R("rstd")], writes=[st.R("mkT", idx)])

    st.linear(w_mem_kv, D, [[(m * 128, 128)] for m in range(4)], lambda c: (memt[:, c, :], [st.R("memt")]), evac, wb, [1, 2], n=256)
    p.op("sp", lambda e: e.dma_start(out=wv[:], in_=cv(w_mem_kv)[:, :, 512:1024]), writes=[st.R("wv")], dma="wv")
    for j in range(2):
        ps = st.ps[3 + j]
        for c in range(KC):
            p.op("pe", lambda e, c=c, j=j, ps=ps: e.matmul(ps[:], memt[:, c, j * 128:(j + 1) * 128], wv[:, c, :], start=(c == 0), stop=(c == KC - 1)),
                 reads=[st.R("memt"), st.R("wv")], writes=[("ps", 3 + j)])
        p.op("act", lambda e, j=j, ps=ps: e.activation(out=mv[j][:], in_=ps[:], func=AF.Copy), reads=[("ps", 3 + j)], writes=[st.R("mv", j)])


def mem_attn(st, m, ps, psi, mkT, mv, mem_q_gain_s, memo_dst, t, bufs):
    p = st.p
    mqs, qn, pt, rden, mo = bufs
    tsl = slice(t * TT, (t + 1) * TT)
    p.op("act", lambda e: e.activation(out=mqs[:], in_=ps[:], func=AF.Copy), reads=[("ps", psi)], writes=[st.R("mqs")])
    st.rms([(mqs[:], 128, [st.R("mqs")])], 128, 7)
    p.op("dve", lambda e: e.scalar_tensor_tensor(out=qn[:], in0=mqs[:], scalar=mem_q_gain_s[:, 0:1], in1=st.rstd[:], op0=ALU.mult, op1=ALU.mult),
         reads=[st.R("mqs"), st.R("mem_q_gain"), st.R("rstd")], writes=[st.R("qn")])
    for j in range(2):
        p.op("pe", lambda e, j=j: e.matmul(st.ps[3 + j][:], mkT[m][:, j * 128:(j + 1) * 128], qn[:], start=True, stop=True),
             reads=[st.R("mkT", m), st.R("qn")], writes=[("ps", 3 + j)])
        p.op("act", lambda e, j=j: e.activation(out=pt[j][:], in_=st.ps[3 + j][:], func=AF.Exp, scale=128.0 ** -0.5),
             reads=[("ps", 3 + j)], writes=[st.R("pt", j)])
    for j in range(2):
        p.op("pe", lambda e, j=j: e.matmul(st.ps[5][:], mv[j][:, m * 128:(m + 1) * 128], pt[j][:], start=(j == 0), stop=(j == 1)),
             reads=[st.R("mv", j), st.R("pt", j)], writes=[("ps", 5)])
    for j in range(2):
        p.op("pe", lambda e, j=j: e.matmul(st.ps[6][:], st.ones[:], pt[j][:], start=(j == 0), stop=(j == 1)),
             reads=[st.R("ones"), st.R("pt", j)], writes=[("ps", 6)])
    p.op("dve", lambda e: e.reciprocal(out=rden[:], in_=st.ps[6][:]), reads=[("ps", 6)], writes=[st.R("rden")])
    p.op("dve", lambda e: e.tensor_tensor(out=mo[:], in0=st.ps[5][:], in1=rden[:], op=ALU.mult),
         reads=[("ps", 5), st.R("rden")], writes=[st.R("mo")])
    st.store(memo_dst[m * 128:(m + 1) * 128, tsl], mo[:], [st.R("mo")], [("memo", t, m)], "mo")


def mixin_stage(cx, kind, xsrc, W, gains, memT, w_mem_kv, outs, extra):
    st = Stage(cx, "mix" + kind)
    p = st.p
    st.consts()
    ht = st.sb("ht", [128, KC, TT], F32R)
    gain = st.load_small("gain", gains["mix_gain"], [128, KC])
    mem_gain_s = st.load_small("mem_gain", gains["mem_gain"], [128, KC])
    mem_q_gain_s = st.load_small("mem_q_gain", gains["mem_q_gain"], [128, 1])
    mem_k_gain_s = st.load_small("mem_k_gain", gains["mem_k_gain"], [128, 1])
    wb = [st.sb(f"wb{i}", [128, KC, 128], F32R) for i in range(3)]
    memt = st.sb("memt", [128, KC, 256], F32R)
    wv = st.sb("wv", [128, KC, 512], F32R)
    mkT = [st.sb(f"mkT{m}", [128, 256], F32R) for m in range(4)]
    mv = [st.sb(f"mv{j}", [128, 512], F32R) for j in range(2)]
    mem_prep(st, memT, mem_gain_s, w_mem_kv, mem_k_gain_s, memt, mkT, mv, wb, wv)
    bufs = (st.sb("mqs", [128, TT]), st.sb("qn", [128, TT], F32R), [st.sb(f"pt{j}", [128, TT], F32R) for j in range(2)],
            st.sb("rden", [128, TT]), st.sb("mo", [128, TT]))
    stg = [st.sb(f"stg{i}", [128, TT]) for i in range(3)]

    def stage_out(ps, psi, M, dst, wres, func=AF.Copy, bias=None):
        b = st.rot("stg", 3)
        eng = "act" if (bias is not None or st.rot("ev", 2) == 0) else "dve"
        if eng == "act":
            if bias is not None:
                p.op("act", lambda e: e.activation(out=stg[b][0:M, :], in_=ps[0:M, :], func=AF.Identity, bias=bias),
                     reads=[("ps", psi), st.R("bg")], writes=[st.R("stg", b)])
            else:
                p.op("act", lambda e: e.activation(out=stg[b][0:M, :], in_=ps[0:M, :], func=func), reads=[("ps", psi)], writes=[st.R("stg", b)])
        else:
            p.op("dve", lambda e: e.tensor_copy(out=stg[b][0:M, :], in_=ps[0:M, :]), reads=[("ps", psi)], writes=[st.R("stg", b)])
        st.store(dst, stg[b][0:M, :], [st.R("stg", b)], wres, f"stg{b}", q="pool")

    if kind == "A":
        bg = st.load_small("bg", extra["b_gates"], [8, 1])
        cols = [[(c * 128, 128)] for c in range(36)] + [[(4608, 8)]] + [[(4616 + m * 128, 128)] for m in range(4)]
        for t in range(NTOK // TT):
            tsl = slice(t * TT, (t + 1) * TT)
            load_norm_h(st, ht, xsrc, gain, t)

            def evac(idx, ps, M, psi, t=t, tsl=tsl):
                if idx < 12:
                    stage_out(ps, psi, 128, outs["qkT"][idx * 128:(idx + 1) * 128, tsl], [("qkT", t, idx)])
                elif idx < 24:
                    i = idx - 12
                    stage_out(ps, psi, 128, outs["vT"][i * 128:(i + 1) * 128, tsl], [("vT", t, i)])
                elif idx < 36:
                    i = idx - 24
                    stage_out(ps, psi, 128, outs["oT"][i * 128:(i + 1) * 128, tsl], [("oT", t, i)])
                elif idx == 36:
                    stage_out(ps, psi, 8, outs["gT"][:, tsl], [("gT", t)], bias=bg[0:8, 0:1])
                else:
                    mem_attn(st, idx - 37, ps, psi, mkT, mv, mem_q_gain_s, outs["memoT"], t, bufs)

            st.linear(W, D, cols, lambda c: (ht[:, c, :], [st.R("ht")]), evac, wb, [1, 2])
    else:
        qlg = st.load_small("qlg", gains["q_latent_gain"], [128, 4])
        qg = st.load_small("qg", gains["q_gain"], [128, 3])
        cosF = st.sb("cosF", [64, NTOK])
        sinS = st.sb("sinS", [64, NTOK])
        rope_tables(st, extra["posb"], extra["invf"], cosF, sinS)
        cqs = st.sb("cqs", [128, 4, TT])
        cqn = st.sb("cqn", [128, 4, TT], F32R)
        wq = [st.sb(f"wq{i}", [128, 4, 128], F32R) for i in range(3)]
        qhn = st.sb("qhn", [128, TT])
        qhr = st.sb("qhr", [64, TT])
        qhs = st.sb("qhs", [64, TT])
        ra = st.sb("ra", [64, TT])
        rb = st.sb("rb", [64, TT])
        W_uq = extra["w_uq"]
        cols = [[(0, 128)], [(128, 128)], [(256, 128)], [(384, 64)]] + [[(448 + m * 128, 128)] for m in range(4)]
        for t in range(NTOK // TT):
            tsl = slice(t * TT, (t + 1) * TT)
            load_norm_h(st, ht, xsrc, gain, t)

            def evac(idx, ps, M, psi, t=t, tsl=tsl):
                if idx < 4:
                    p.op("act", lambda e: e.activation(out=cqs[0:M, idx, :], in_=ps[0:M, :], func=AF.Copy),
                         reads=[("ps", psi)], writes=[st.R("cqs", idx)])
                else:
                    mem_attn(st, idx - 4, ps, psi, mkT, mv, mem_q_gain_s, outs["memoT"], t, bufs)

            st.linear(W, D, cols, lambda c: (ht[:, c, :], [st.R("ht")]), evac, wb, [1, 2])
            rows = [128, 128, 128, 64]
            st.rms([(cqs[0:rows[c], c, :], rows[c], [st.R("cqs", c)]) for c in range(4)], 448, 7)
            for c in range(4):
                p.op("dve", lambda e, c=c: e.scalar_tensor_tensor(out=cqn[0:rows[c], c, :], in0=cqs[0:rows[c], c, :], scalar=qlg[0:rows[c], c:c + 1],
                                                                  in1=st.rstd[0:rows[c], :], op0=ALU.mult, op1=ALU.mult),
                     reads=[st.R("cqs", c), st.R("qlg"), st.R("rstd")], writes=[st.R("cqn", c)])
            for hd in range(12):
                b0 = hd * 192
                hcols = [[(b0, 128)], [(b0 + 128, 64)], [(b0 + 160, 32), (b0 + 128, 32)]]

                def evac2(idx, ps, M, psi, hd=hd, t=t, tsl=tsl):
                    dstb = [qhn, qhr, qhs][idx]
                    p.op("act", lambda e: e.activation(out=dstb[0:M, :], in_=ps[0:M, :], func=AF.Copy),
                         reads=[("ps", psi)], writes=[st.R("qh", idx)])

                st.linear(W_uq, 448, hcols, lambda c: (cqn[0:rows[c], c, :], [st.R("cqn", c)]), evac2, wq, [1, 2])
                st.rms([(qhn[:], 128, [st.R("qh", 0)]), (qhr[:], 64, [st.R("qh", 1)])], 192, 7)
                b = st.rot("stg", 3)
                p.op("dve", lambda e, b=b: e.scalar_tensor_tensor(out=stg[b][:], in0=qhn[:], scalar=qg[:, 0:1], in1=st.rstd[:], op0=ALU.mult, op1=ALU.mult),
                     reads=[st.R("qh", 0), st.R("qg"), st.R("rstd")], writes=[st.R("stg", b)])
                st.store(outs["QT"][hd, 0:128, tsl], stg[b][:], [st.R("stg", b)], [("QT", t, hd, 0)], f"stg{b}", q="pool")
                p.op("dve", lambda e: e.scalar_tensor_tensor(out=ra[:], in0=qhr[:], scalar=qg[0:64, 1:2], in1=st.rstd[0:64, :], op0=ALU.mult, op1=ALU.mult),
                     reads=[st.R("qh", 1), st.R("qg"), st.R("rstd")], writes=[st.R("ra")])
                p.op("dve", lambda e: e.scalar_tensor_tensor(out=rb[:], in0=qhs[:], scalar=qg[0:64, 2:3], in1=st.rstd[0:64, :], op0=ALU.mult, op1=ALU.mult),
                     reads=[st.R("qh", 2), st.R("qg"), st.R("rstd")], writes=[st.R("rb")])
                p.op("pool", lambda e, tsl=tsl: e.tensor_tensor(out=ra[:], in0=ra[:], in1=cosF[:, tsl], op=ALU.mult),
                     reads=[st.R("ra"), st.R("cosF")], writes=[st.R("ra")])
                p.op("pool", lambda e, tsl=tsl: e.tensor_tensor(out=rb[:], in0=rb[:], in1=sinS[:, tsl], op=ALU.mult),
                     reads=[st.R("rb"), st.R("sinS")], writes=[st.R("rb")])
                b = st.rot("stg", 3)
                p.op("dve", lambda e, b=b: e.tensor_tensor(out=stg[b][0:64, :], in0=ra[:], in1=rb[:], op=ALU.add),
                     reads=[st.R("ra"), st.R("rb")], writes=[st.R("stg", b)])
                st.store(outs["QT"][hd, 128:192, tsl], stg[b][0:64, :], [st.R("stg", b)], [("QT", t, hd, 1)], f"stg{b}", q="pool")
    st.close()


def sharedkv_stage(cx, xsrc, gains, w_dkv, w_ukv, extra, outs):
    st = Stage(cx, "skv")
    p = st.p
    st.consts()
    ht = st.sb("ht", [128, KC, TT], F32R)
    gain = st.load_small("gain", gains["kv_gain"], [128, KC])
    klg = st.load_small("klg", gains["kv_latent_gain"], [128, 4])
    kg = st.load_small("kg", gains["k_gain"], [128, 3])
    cosF = st.sb("cosF", [64, NTOK])
    sinS = st.sb("sinS", [64, NTOK])
    rope_tables(st, extra["posb"], extra["invf"], cosF, sinS)
    wb = [st.sb(f"wb{i}", [128, KC, 128], F32R) for i in range(3)]
    wq = [st.sb(f"wq{i}", [128, 4, 128], F32R) for i in range(3)]
    cks = st.sb("cks", [128, 4, TT])
    ckn = st.sb("ckn", [128, 4, TT], F32R)
    kpr = st.sb("kpr", [64, TT])
    kps = st.sb("kps", [64, TT])
    sqpe = st.sb("sqpe", [64, TT], F32R)
    Rk = st.sb("Rk", [64, TT])
    rb = st.sb("rb", [64, TT])
    kn = st.sb("kn", [128, TT])
    sqn = st.sb("sqn", [128, TT], F32R)
    stg = [st.sb(f"stg{i}", [128, TT]) for i in range(3)]
    cols = [[(c * 128, 128)] for c in range(4)] + [[(512, 64)], [(544, 32), (512, 32)]]
    for t in range(NTOK // TT):
        tsl = slice(t * TT, (t + 1) * TT)
        load_norm_h(st, ht, xsrc, gain, t)

        def evac(idx, ps, M, psi):
            if idx < 4:
                p.op("act", lambda e: e.activation(out=cks[:, idx, :], in_=ps[:], func=AF.Copy), reads=[("ps", psi)], writes=[st.R("cks", idx)])
            else:
                dstb = kpr if idx == 4 else kps
                p.op("act", lambda e: e.activation(out=dstb[:], in_=ps[0:64, :], func=AF.Copy), reads=[("ps", psi)], writes=[st.R("kp", idx)])

        st.linear(w_dkv, D, cols, lambda c: (ht[:, c, :], [st.R("ht")]), evac, wb, [1, 2])
        st.rms([(cks[:, c, :], 128, [st.R("cks", c)]) for c in range(4)], 512, 7)
        for c in range(4):
            p.op("dve", lambda e, c=c: e.scalar_tensor_tensor(out=ckn[:, c, :], in0=cks[:, c, :], scalar=klg[:, c:c + 1], in1=st.rstd[:],
                                                              op0=ALU.mult, op1=ALU.mult),
                 reads=[st.R("cks", c), st.R("klg"), st.R("rstd")], writes=[st.R("ckn", c)])
        p.op("act", lambda e: e.activation(out=sqpe[:], in_=kpr[:], func=AF.Square), reads=[st.R("kp", 4)], writes=[st.R("sqpe")])
        p.op("dve", lambda e, tsl=tsl: e.scalar_tensor_tensor(out=Rk[:], in0=kpr[:], scalar=kg[0:64, 1:2], in1=cosF[:, tsl], op0=ALU.mult, op1=ALU.mult),
             reads=[st.R("kp", 4), st.R("kg"), st.R("cosF")], writes=[st.R("Rk")])
        p.op("dve", lambda e, tsl=tsl: e.scalar_tensor_tensor(out=rb[:], in0=kps[:], scalar=kg[0:64, 2:3], in1=sinS[:, tsl], op0=ALU.mult, op1=ALU.mult),
             reads=[st.R("kp", 5), st.R("kg"), st.R("sinS")], writes=[st.R("rb")])
        p.op("dve", lambda e: e.tensor_tensor(out=Rk[:], in0=Rk[:], in1=rb[:], op=ALU.add), reads=[st.R("Rk"), st.R("rb")], writes=[st.R("Rk")])
        for hd in range(12):
            hcols = [[(hd * 256, 128)], [(hd * 256 + 128, 128)]]

            def evac2(idx, ps, M, psi, hd=hd, t=t, tsl=tsl):
                if idx == 0:
                    p.op("act", lambda e: e.activation(out=kn[:], in_=ps[:], func=AF.Copy), reads=[("ps", psi)], writes=[st.R("kn")])
                else:
                    b = st.rot("stg", 3)
                    p.op("dve", lambda e, b=b: e.tensor_copy(out=stg[b][:], in_=ps[:]), reads=[("ps", psi)], writes=[st.R("stg", b)])
                    st.store(outs["VT"][hd, :, tsl], stg[b][:], [st.R("stg", b)], [("VT", t, hd)], f"stg{b}", q="pool")

            st.linear(w_ukv, 512, hcols, lambda c: (ckn[:, c, :], [st.R("ckn", c)]), evac2, wq, [1, 2])
            p.op("act", lambda e: e.activation(out=sqn[:], in_=kn[:], func=AF.Square), reads=[st.R("kn")], writes=[st.R("sqn")])
            p.op("pe", lambda e: e.matmul(st.ps[7][:], st.ones[:], sqn[:], start=True, stop=False),
                 reads=[st.R("ones"), st.R("sqn")], writes=[("ps", 7)])
            p.op("pe", lambda e: e.matmul(st.ps[7][:], st.ones[0:64, :], sqpe[:], start=False, stop=True),
                 reads=[st.R("ones"), st.R("sqpe")], writes=[("ps", 7)])
            p.op("act", lambda e: e.activation(out=st.rstd[:], in_=st.ps[7][:], func=AF.Sqrt, bias=st.epsb[:], scale=1.0 / 192),
                 reads=[("ps", 7), st.R("epsb")], writes=[st.R("rstd")])
            p.op("dve", lambda e: e.reciprocal(out=st.rstd[:], in_=st.rstd[:]), reads=[st.R("rstd")], writes=[st.R("rstd")])
            b = st.rot("stg", 3)
            p.op("dve", lambda e, b=b: e.scalar_tensor_tensor(out=stg[b][:], in0=kn[:], scalar=kg[:, 0:1], in1=st.rstd[:], op0=ALU.mult, op1=ALU.mult),
                 reads=[st.R("kn"), st.R("kg"), st.R("rstd")], writes=[st.R("stg", b)])
            st.store(outs["KT"][hd, 0:128, tsl], stg[b][:], [st.R("stg", b)], [("KT", t, hd, 0)], f"stg{b}", q="pool")
            b = st.rot("stg", 3)
            p.op("dve", lambda e, b=b: e.tensor_tensor(out=stg[b][0:64, :], in0=Rk[:], in1=st.rstd[0:64, :], op=ALU.mult),
                 reads=[st.R("Rk"), st.R("rstd")], writes=[st.R("stg", b)])
            st.store(outs["KT"][hd, 128:192, tsl], stg[b][0:64, :], [st.R("stg", b)], [("KT", t, hd, 1)], f"stg{b}", q="pool")
    st.close()


def mixout_stage(cx, xsrc, xdst, mainT, memoT, oT, w_out):
    st = Stage(cx, "mo")
    p = st.p
    ht = st.sb("ht", [128, KC, TT], F32R)
    og = st.sb("og", [128, 12, TT]) if oT is not None else None
    wo = [st.sb(f"wo{i}", [128, KC, 128], F32R) for i in range(3)]
    xr = [st.sb(f"xr{i}", [128, TT]) for i in range(2)]
    xo = [st.sb(f"xo{i}", [128, TT]) for i in range(2)]
    for t in range(NTOK // TT):
        tsl = slice(t * TT, (t + 1) * TT)
        p.op("sp", lambda e, tsl=tsl: e.dma_start(out=ht[:, 0:12, :], in_=cv(mainT)[:, :, tsl]), writes=[st.R("htm")], dma="htm")
        p.op("sp", lambda e, tsl=tsl: e.dma_start(out=ht[:, 12:16, :], in_=cv(memoT)[:, :, tsl]), writes=[st.R("hte")], dma="hte")
        if oT is not None:
            p.op("sp", lambda e, tsl=tsl: e.dma_start(out=og[:], in_=cv(oT)[:, :, tsl]), writes=[st.R("og")], dma="og")
            p.op("act", lambda e: e.activation(out=og[:], in_=og[:], func=AF.Sigmoid), reads=[st.R("og")], writes=[st.R("og")])
            for c in range(12):
                p.op("dve", lambda e, c=c: e.tensor_tensor(out=ht[:, c, :], in0=ht[:, c, :].bitcast(F32), in1=og[:, c, :], op=ALU.mult),
                     reads=[st.R("htm"), st.R("og")], writes=[st.R("htm")])
        outproj(st, w_out, KC, lambda j: (ht[:, j, :], [st.R("htm") if j < 12 else st.R("hte")]), wo, xsrc, xdst, t, 1.0, xr, xo)
    st.close()


def seqmix_program(mode):
    cx = Ctx()
    nc, p = cx.nc, cx.p
    NH = 3 if mode == "mla" else 1
    DV = 128 if mode == "mla" else 384
    NV = DV // 128
    masks_d = cx.din("masks", [4, 128, TT], F32)
    if mode == "mla":
        QT = cx.din("QT", [NH, 192, SEQ])
        KT = cx.din("KT", [NH, 192, SEQ])
        V = cx.din("V", [NH, SEQ, DV])
        OT = cx.dout("OT", [NH, DV, SEQ])
    else:
        QP = cx.din("QP", [192, SEQ + 3], F32)
        KP = cx.din("KP", [192, SEQ + 3], F32)
        V = cx.din("V", [1, SEQ, DV])
        IG = cx.din("IG", [128, 64], F32)
        FG = cx.din("FG", [128, 64], F32)
        CW = cx.din("CW", [128, 4, 4], F32)
        HG = cx.din("HG", [128, 3], F32)
        IDN = cx.din("IDN", [128, 128], F32)
        LTRI = cx.din("LTRI", [128, 128], F32)
        USTR = cx.din("USTR", [64, 64], F32)
        OT = cx.dout("OT", [1, DV, SEQ])
    st = Stage(cx, "sm")
    st.consts()
    masks = st.sb("masks", [128, 4, TT])
    p.op("sp", lambda e: e.dma_start(out=masks[:], in_=masks_d.rearrange("j p t -> p j t")), writes=[st.R("masks")], dma="masks")
    kT_hi = st.sb("kT_hi", [128, SEQ], F32R)
    kT_lo = st.sb("kT_lo", [64, SEQ], F32R)
    q_hi = [st.sb(f"q_hi{i}", [128, TT], F32R) for i in range(2)]
    q_lo = [st.sb(f"q_lo{i}", [64, TT], F32R) for i in range(2)]
    vt = [st.sb(f"vt{i}", [128, DV], F32R) for i in range(3)]
    wT = [st.sb(f"wT{i}", [128, TT], F32R) for i in range(2)]
    et = [st.sb(f"et{i}", [128, TT]) for i in range(2)]
    rden = st.sb("rden", [128, TT])
    ho = [st.sb(f"ho{i}", [128, TT]) for i in range(NV)]
    NUM0 = 2
    if mode == "mlstm":
        HS = SEQ // 2
        tmp = st.sb("tmp", [128, HS + 3])
        acc = st.sb("acc", [128, HS])
        cw = st.load_small("cw", CW, [128, 4, 4])
        hg = st.load_small("hg", HG, [128, 3])
        idn = st.load_small("idn", IDN, [128, 128])
        ltri = st.load_small("ltri", LTRI, [128, 128])
        ustr = st.load_small("ustr", USTR, [64, 64])
        ig = st.load_small("ig", IG, [128, 64])
        fg = st.load_small("fg", FG, [128, 64])
        Bbc = st.sb("Bbc", [128, SEQ])
        bias = st.sb("bias", [128, 64])
        lf = st.sb("lf", [128, 64])
        bb = st.sb("bb", [128, 64])
        totbc = st.sb("totbc", [64, 128])
        dg = [st.sb(f"dg{i}", [128, 128]) for i in range(2)]

        def conv(src_ap, rows, n, widx, dst_ap, scale, rk_src, rk_dst):
            a = acc[0:rows, 0:n]
            p.op("dve", lambda e: e.tensor_scalar(out=a, in0=src_ap[:, 3:3 + n], scalar1=cw[0:rows, widx, 3:4], scalar2=None, op0=ALU.mult),
                 reads=[rk_src, st.R("cw")], writes=[st.R("acc")])
            for j in range(3):
                p.op("dve", lambda e, j=j: e.scalar_tensor_tensor(out=a, in0=src_ap[:, j:j + n], scalar=cw[0:rows, widx, j:j + 1], in1=a,
                                                                  op0=ALU.mult, op1=ALU.add),
                     reads=[rk_src, st.R("cw"), st.R("acc")], writes=[st.R("acc")])
            p.op("act", lambda e: e.activation(out=a, in_=a, func=AF.Silu), reads=[st.R("acc")], writes=[st.R("acc")])
            p.op("act", lambda e: e.activation(out=dst_ap, in_=a, func=AF.Copy, scale=float(scale)), reads=[st.R("acc")], writes=[rk_dst])

        for (r0, rows, dstt, widx) in ((0, 128, kT_hi, 2), (128, 64, kT_lo, 3)):
            for hh in range(2):
                p.op("sp", lambda e, r0=r0, rows=rows, hh=hh: e.dma_start(out=tmp[0:rows, :], in_=KP[r0:r0 + rows, hh * HS:(hh + 1) * HS + 3]),
                     writes=[st.R("tmp")], dma="tmp")
                conv(tmp[0:rows, :], rows, HS, widx, dstt[:, hh * HS:(hh + 1) * HS], 1.0, st.R("tmp"), st.R("kT", r0, hh))
        p.op("act", lambda e: e.activation(out=lf[:], in_=fg[:], func=AF.Exp, scale=-1.0), reads=[st.R("fg")], writes=[st.R("lf")])
        p.op("act", lambda e: e.activation(out=lf[:], in_=lf[:], func=AF.Ln, bias=st.onesf[:, 0:1], scale=1.0),
             reads=[st.R("lf"), st.R("onesf")], writes=[st.R("lf")])
        p.op("dve", lambda e: e.tensor_scalar(out=lf[:], in0=lf[:], scalar1=-1.0, scalar2=None, op0=ALU.mult), reads=[st.R("lf")], writes=[st.R("lf")])
        p.op("pe", lambda e: e.matmul(st.ps[6][0:64, 0:128], lf[:], st.onesf[:], start=True, stop=True),
             reads=[st.R("lf"), st.R("onesf")], writes=[("ps", 6)])
        p.op("dve", lambda e: e.tensor_copy(out=totbc[:], in_=st.ps[6][0:64, 0:128]), reads=[("ps", 6)], writes=[st.R("totbc")])
        p.op("pe", lambda e: e.matmul(st.ps[7][:, 0:64], ltri[:], lf[:], start=True, stop=False),
             reads=[st.R("ltri"), st.R("lf")], writes=[("ps", 7)])
        p.op("pe", lambda e: e.matmul(st.ps[7][:, 0:64], totbc[:], ustr[:], start=False, stop=True),
             reads=[st.R("totbc"), st.R("ustr")], writes=[("ps", 7)])
        p.op("dve", lambda e: e.tensor_copy(out=bb[:], in_=st.ps[7][:, 0:64]), reads=[("ps", 7)], writes=[st.R("bb")])
        p.op("dve", lambda e: e.tensor_tensor(out=bias[:], in0=ig[:], in1=bb[:], op=ALU.subtract), reads=[st.R("ig"), st.R("bb")], writes=[st.R("bias")])
        for g in range(16):
            for k in range(4):
                blk = g * 4 + k
                d = st.rot("dg", 2)
                p.op("dve", lambda e, blk=blk, d=d: e.tensor_scalar(out=dg[d][:], in0=idn[:], scalar1=bb[:, blk:blk + 1], scalar2=None, op0=ALU.mult),
                     reads=[st.R("idn"), st.R("bb")], writes=[st.R("dg", d)])
                p.op("pe", lambda e, k=k, d=d: e.matmul(st.ps[6][:, k * 128:(k + 1) * 128], st.onesf[:], dg[d][:], start=True, stop=True),
                     reads=[st.R("onesf"), st.R("dg", d)], writes=[("ps", 6)])
            p.op("act", lambda e, g=g: e.activation(out=Bbc[:, g * TT:(g + 1) * TT], in_=st.ps[6][:], func=AF.Copy),
                 reads=[("ps", 6)], writes=[st.R("Bbc", g)])

    for hd in range(NH):
        if mode == "mla":
            p.op("sp", lambda e, hd=hd: e.dma_start(out=kT_hi[:], in_=KT[hd, 0:128, :]), writes=[st.R("kT", 0)], dma="kthi")
            p.op("sp", lambda e, hd=hd: e.dma_start(out=kT_lo[:], in_=KT[hd, 128:192, :]), writes=[st.R("kT", 128)], dma="ktlo")
        for qt in range(SEQ // TT):
            qsl = slice(qt * TT, (qt + 1) * TT)
            qb = st.rot("q", 2)
            if mode == "mla":
                p.op("sp", lambda e, hd=hd, qsl=qsl, qb=qb: e.dma_start(out=q_hi[qb][:], in_=QT[hd, 0:128, qsl]), writes=[st.R("qh", qb)], dma=f"qh{qb}")
                p.op("sp", lambda e, hd=hd, qsl=qsl, qb=qb: e.dma_start(out=q_lo[qb][:], in_=QT[hd, 128:192, qsl]), writes=[st.R("ql", qb)], dma=f"ql{qb}")
            else:
                for (r0, rows, dstt, widx, rkn) in ((0, 128, q_hi[qb], 0, "qh"), (128, 64, q_lo[qb], 1, "ql")):
                    p.op("sp", lambda e, r0=r0, rows=rows, qt=qt: e.dma_start(out=tmp[0:rows, 0:TT + 3], in_=QP[r0:r0 + rows, qt * TT:(qt + 1) * TT + 3]),
                         writes=[st.R("tmp")], dma="tmp")
                    conv(tmp[0:rows, 0:TT + 3], rows, TT, widx, dstt[:], 192.0 ** -0.5, st.R("tmp"), st.R(rkn, qb))
            nst = 4 * (qt + 1)
            for s in range(nst):
                ssl = slice(s * 128, (s + 1) * 128)
                sb_ = st.rot("S", 2)
                psS = st.ps[sb_]
                p.op("pe", lambda e, ssl=ssl, qb=qb, psS=psS: e.matmul(psS[:], kT_hi[:, ssl], q_hi[qb][:], start=True, stop=False),
                     reads=[st.R("kT", 0), st.R("kT", 0, 0), st.R("kT", 0, 1), st.R("qh", qb)], writes=[("ps", sb_)])
                p.op("pe", lambda e, ssl=ssl, qb=qb, psS=psS: e.matmul(psS[:], kT_lo[:, ssl], q_lo[qb][:], start=False, stop=True),
                     reads=[st.R("kT", 128), st.R("kT", 128, 0), st.R("kT", 128, 1), st.R("ql", qb)], writes=[("ps", sb_)])
                vb = st.rot("v", 3)
                p.op("act", lambda e, hd=hd, ssl=ssl, vb=vb: e.dma_start(out=vt[vb][:], in_=V[hd, ssl, :]), writes=[st.R("vt", vb)], dma=f"vt{vb}")
                wbi = st.rot("wT", 2)
                dj = s - 4 * qt
                if mode == "mla":
                    if dj >= 0:
                        p.op("dve", lambda e, wbi=wbi, dj=dj, psS=psS: e.scalar_tensor_tensor(out=et[wbi][:], in0=psS[:], scalar=192.0 ** -0.5,
                                                                                         in1=masks[:, dj, :], op0=ALU.mult, op1=ALU.add),
                             reads=[("ps", sb_), st.R("masks")], writes=[st.R("et", wbi)])
                        p.op("act", lambda e, wbi=wbi: e.activation(out=wT[wbi][:], in_=et[wbi][:], func=AF.Exp),
                             reads=[st.R("et", wbi)], writes=[st.R("wT", wbi)])
                    else:
                        p.op("act", lambda e, wbi=wbi, psS=psS: e.activation(out=wT[wbi][:], in_=psS[:], func=AF.Exp, scale=192.0 ** -0.5),
                             reads=[("ps", sb_)], writes=[st.R("wT", wbi)])
                else:
                    if dj >= 0:
                        p.op("pool", lambda e, wbi=wbi, dj=dj, qsl=qsl: e.tensor_tensor(out=et[wbi][:], in0=Bbc[:, qsl], in1=masks[:, dj, :], op=ALU.add),
                             reads=[st.R("Bbc", qt), st.R("masks")], writes=[st.R("et", wbi)])
                        p.op("act", lambda e, wbi=wbi, s=s: e.activation(out=et[wbi][:], in_=et[wbi][:], func=AF.Exp, bias=bias[:, s:s + 1], scale=1.0),
                             reads=[st.R("et", wbi), st.R("bias")], writes=[st.R("et", wbi)])
                    else:
                        p.op("act", lambda e, wbi=wbi, s=s, qsl=qsl: e.activation(out=et[wbi][:], in_=Bbc[:, qsl], func=AF.Exp, bias=bias[:, s:s + 1], scale=1.0),
                             reads=[st.R("Bbc", qt), st.R("bias")], writes=[st.R("et", wbi)])
                    p.op("dve", lambda e, wbi=wbi, psS=psS: e.tensor_tensor(out=wT[wbi][:], in0=psS[:], in1=et[wbi][:], op=ALU.mult),
                         reads=[("ps", sb_), st.R("et", wbi)], writes=[st.R("wT", wbi)])
                for c in range(NV):
                    p.op("pe", lambda e, c=c, vb=vb, wbi=wbi, s=s: e.matmul(st.ps[NUM0 + c][:], vt[vb][:, c * 128:(c + 1) * 128], wT[wbi][:],
                                                                           start=(s == 0), stop=(s == nst - 1)),
                         reads=[st.R("vt", vb), st.R("wT", wbi)], writes=[("ps", NUM0 + c)])
                p.op("pe", lambda e, wbi=wbi, s=s: e.matmul(st.ps[5][:], st.ones[:], wT[wbi][:], start=(s == 0), stop=(s == nst - 1)),
                     reads=[st.R("ones"), st.R("wT", wbi)], writes=[("ps", 5)])
            if mode == "mla":
                p.op("dve", lambda e: e.reciprocal(out=rden[:], in_=st.ps[5][:]), reads=[("ps", 5)], writes=[st.R("rden")])
            else:
                p.op("act", lambda e: e.activation(out=rden[:], in_=st.ps[5][:], func=AF.Abs), reads=[("ps", 5)], writes=[st.R("rden")])
                p.op("dve", lambda e: e.tensor_scalar(out=rden[:], in0=rden[:], scalar1=1.0, scalar2=None, op0=ALU.max),
                     reads=[st.R("rden")], writes=[st.R("rden")])
                p.op("dve", lambda e: e.reciprocal(out=rden[:], in_=rden[:]), reads=[st.R("rden")], writes=[st.R("rden")])
            for c in range(NV):
                p.op("dve", lambda e, c=c: e.tensor_tensor(out=ho[c][:], in0=st.ps[NUM0 + c][:], in1=rden[:], op=ALU.mult),
                     reads=[("ps", NUM0 + c), st.R("rden")], writes=[st.R("ho", c)])
            if mode == "mlstm":
                st.rms([(ho[c][:], 128, [st.R("ho", c)]) for c in range(NV)], DV, 7)
                for c in range(NV):
                    p.op("dve", lambda e, c=c: e.scalar_tensor_tensor(out=ho[c][:], in0=ho[c][:], scalar=hg[:, c:c + 1], in1=st.rstd[:],
                                                                      op0=ALU.mult, op1=ALU.mult),
                         reads=[st.R("ho", c), st.R("hg"), st.R("rstd")], writes=[st.R("ho", c)])
            for c in range(NV):
                p.op("pool", lambda e, hd=hd, c=c, qsl=qsl: e.dma_start(out=OT[hd, c * 128:(c + 1) * 128, qsl], in_=ho[c][:]),
                     reads=[st.R("ho", c)], writes=[("OT", hd, c, qt)], dma=f"ot{c}")
                cx.outres.append(("OT", hd, c, qt))
    p.final_wait("sp", cx.outres)
    st.close()
    return cx


def gains_in(cx, names):
    g = {}
    for n, shp in names.items():
        g[n] = cx.din("g_" + n, shp, F32)
    return g


def tprog(idx):
    cx = Ctx()
    xin = cx.din("xin", [D, NTOK], F32)
    xs = cx.dout("xs", [D, NTOK])
    first = [True]

    pos_c = []

    def posin():
        if not pos_c:
            pos_c.append((cx.din("posb", [64, NTOK], I32), cx.din("invf", [64, 2], F32)))
        return pos_c[0]

    def src():
        s = xin if first[0] else xs
        first[0] = False
        return s

    def ffn(tag):
        w_in = cx.din(tag + "_w_in", [D, 2 * DFF])
        w_out = cx.din(tag + "_w_out", [DFF, D])
        g = cx.din(tag + "_gain", [128, KC], F32)
        ffn_stage(cx, src(), xs, w_in, w_out, g)

    def mixout(kindA):
        mainT = cx.din("mainT", [1536, NTOK])
        memoTi = cx.din("memoT_in", [512, NTOK])
        oT = cx.din("oT_in", [1536, NTOK], F32) if kindA else None
        w_out = cx.din("w_out", [D, D])
        mixout_stage(cx, src(), xs, mainT, memoTi, oT, w_out)

    def mixin(kind):
        memT = cx.din("memT", [D, 256])
        w_mem_kv = cx.din("w_mem_kv", [D, 1024])
        outs = {"memoT": cx.dout("memoT", [512, NTOK])}
        names = {"mix_gain": [128, KC], "mem_gain": [128, KC], "mem_q_gain": [128, 1], "mem_k_gain": [128, 1]}
        extra = {}
        if kind == "A":
            W = cx.din("a_w_in", [D, 5128])
            extra["b_gates"] = cx.din("b_gates", [8, 1], F32)
            outs["qkT"] = cx.dout("qkT", [1536, NTOK])
            outs["vT"] = cx.dout("vT", [1536, NTOK])
            outs["oT"] = cx.dout("oT", [1536, NTOK])
            outs["gT"] = cx.dout("gT", [8, NTOK])
        else:
            W = cx.din("b_w_in", [D, 960])
            extra["w_uq"] = cx.din("b_w_uq", [448, 2304])
            extra["posb"], extra["invf"] = posin()
            names["q_latent_gain"] = [128, 4]
            names["q_gain"] = [128, 3]
            outs["QT"] = cx.dout("QT", [12, 192, NTOK])
        g = gains_in(cx, names)
        mixin_stage(cx, kind, xs if not first[0] else src(), W, g, memT, w_mem_kv, outs, extra)

    def skv():
        g = gains_in(cx, {"kv_gain": [128, KC], "kv_latent_gain": [128, 4], "k_gain": [128, 3]})
        w_dkv = cx.din("w_dkv", [D, 576])
        w_ukv = cx.din("w_ukv", [512, 3072])
        pb, iv = posin()
        extra = {"posb": pb, "invf": iv}
        outs = {"KT": cx.dout("KT", [12, 192, NTOK]), "VT": cx.dout("VT", [12, 128, NTOK])}
        sharedkv_stage(cx, xs, g, w_dkv, w_ukv, extra, outs)

    if idx == 0:
        ffn("fa"); mixin("A")
    elif idx == 1:
        mixout(True); ffn("fa"); ffn("fb"); mixin("A")
    elif idx == 2:
        mixout(True); ffn("fa"); skv(); ffn("fb"); mixin("B")
    elif idx == 3:
        mixout(False); ffn("fa"); ffn("fb"); mixin("B")
    else:
        mixout(False); ffn("fa")
    allres = set()
    for r, w in cx.p.res_w.items():
        if isinstance(r, tuple) and r[0] in ("x", "memo", "qkT", "vT", "oT", "gT", "QT", "KT", "VT"):
            allres.add(r)
    cx.p.final_wait("sp", sorted(allres, key=str))
    cx.p.flush()
    return cx


def _run(cx, in_maps):
    names = set(cx.ins.keys())
    maps = [{k: (np.ascontiguousarray(v) if np.asarray(v).dtype == np.int32 else np.ascontiguousarray(v, dtype=np.float32))
             for k, v in m.items() if k in names} for m in in_maps]
    for m in maps:
        assert set(m.keys()) == names, (sorted(names - set(m.keys())), sorted(set(m.keys()) - names))
    res = run_bass_kernel_spmd(cx.nc, maps, core_ids=list(range(NCORES)))
    return res.results


def colT(v, n=128):
    v = np.asarray(v, np.float32)
    return np.ascontiguousarray(v.reshape(-1, n).T)


def pad_cols(v, rows=128):
    out = np.zeros((rows, len(v)), np.float32)
    for i, a in enumerate(v):
        out[:len(a), i] = a
    return out


def kernel(x, mem, positions, ffn1_gain, ffn1_w_in, ffn1_w_out, mix_gain, w_out, mem_gain, w_mem_kv,
           mem_q_gain, mem_k_gain, a_w_in, a_b_gates, a_conv, a_head_gain, kv_gain, w_dkv, kv_latent_gain,
           w_ukv, k_gain, b_w_in, b_q_latent_gain, b_w_uq, b_q_gain, ffn2_gain, ffn2_w_in, ffn2_w_out,
           _debug=None):
    A = lambda a: np.asarray(a)
    x, mem, positions = A(x).astype(np.float32), A(mem).astype(np.float32), A(positions)
    W = {k: A(v) for k, v in dict(ffn1_gain=ffn1_gain, ffn1_w_in=ffn1_w_in, ffn1_w_out=ffn1_w_out, mix_gain=mix_gain, w_out=w_out,
                                   mem_gain=mem_gain, w_mem_kv=w_mem_kv, mem_q_gain=mem_q_gain, mem_k_gain=mem_k_gain, a_w_in=a_w_in,
                                   a_b_gates=a_b_gates, a_conv=a_conv, a_head_gain=a_head_gain, kv_gain=kv_gain, w_dkv=w_dkv,
                                   kv_latent_gain=kv_latent_gain, w_ukv=w_ukv, k_gain=k_gain, b_w_in=b_w_in,
                                   b_q_latent_gain=b_q_latent_gain, b_w_uq=b_w_uq, b_q_gain=b_q_gain, ffn2_gain=ffn2_gain,
                                   ffn2_w_in=ffn2_w_in, ffn2_w_out=ffn2_w_out).items()}
    cores = range(NCORES)
    bq = [(c // 4, c % 4) for c in cores]
    tok = lambda c: slice(bq[c][1] * NTOK, (bq[c][1] + 1) * NTOK)
    xT = [np.ascontiguousarray(x[bq[c][0], tok(c)].T) for c in cores]
    memT = [np.ascontiguousarray(mem[bq[c][0]].T) for c in cores]
    inv_freq = (10000.0 ** (-np.arange(0, 64, 2, dtype=np.float32) / np.float32(64))).astype(np.float32)
    posb = [np.ascontiguousarray(np.broadcast_to(positions[bq[c][0], tok(c)].astype(np.int32)[None, :], (64, NTOK))) for c in cores]
    invf = np.stack([np.concatenate([inv_freq, inv_freq]), np.concatenate([-np.ones(32, np.float32), np.ones(32, np.float32)])], 1).astype(np.float32)
    masks = np.zeros((4, 128, TT), np.float32)
    for j in range(4):
        pp = np.arange(128)[:, None]
        tt = np.arange(TT)[None, :]
        masks[j] = np.where(tt >= pp + 128 * j, 0.0, MASKV)
    dbg = {}

    def ffn_in(m, tag, which, L):
        m[tag + "_w_in"] = W[which + "_w_in"][L]
        m[tag + "_w_out"] = W[which + "_w_out"][L]
        m[tag + "_gain"] = colT(W[which + "_gain"][L])

    def mix_in(m, c, L, kind):
        m["memT"] = memT[c]
        m["w_mem_kv"] = W["w_mem_kv"][L]
        m["g_mix_gain"] = colT(W["mix_gain"][L])
        m["g_mem_gain"] = colT(W["mem_gain"][L])
        m["g_mem_q_gain"] = W["mem_q_gain"][L].reshape(128, 1)
        m["g_mem_k_gain"] = W["mem_k_gain"][L].reshape(128, 1)
        if kind == "A":
            m["a_w_in"] = W["a_w_in"][L]
            m["b_gates"] = W["a_b_gates"][L].reshape(8, 1)
        else:
            j = L - 2
            m["b_w_in"] = W["b_w_in"][j]
            m["b_w_uq"] = W["b_w_uq"][j]
            m["posb"], m["invf"] = posb[c], invf
            g = W["b_q_latent_gain"][j]
            m["g_q_latent_gain"] = pad_cols([g[0:128], g[128:256], g[256:384], g[384:448]])
            qg = W["b_q_gain"][j]
            m["g_q_gain"] = pad_cols([qg[0:128], qg[128:192], np.concatenate([qg[160:192], qg[128:160]])])

    def seq_full(parts, b):
        return np.concatenate([parts[b * 4 + q] for q in range(4)], axis=1)

    progs = {}

    def getprog(key, fn):
        if key not in progs:
            progs[key] = fn()
        return progs[key]

    maps = []
    for c in cores:
        m = {"xin": xT[c]}
        ffn_in(m, "fa", "ffn1", 0)
        mix_in(m, c, 0, "A")
        maps.append(m)
    if _debug is not None and "T0" in _debug:
        r = _debug["T0"]
    else:
        r = _run(tprog(0), maps)
    if _debug is not None:
        _debug["T0"] = r
        if _debug.get("stop") == "T0":
            return None
    xcur = [r[c]["xs"] for c in cores]

    def mlstm_layer(r, L):
        qkT = [seq_full([r[c]["qkT"] for c in cores], b) for b in range(2)]
        vT = [seq_full([r[c]["vT"] for c in cores], b) for b in range(2)]
        gT = [seq_full([r[c]["gT"] for c in cores], b) for b in range(2)]
        cw = W["a_conv"][L]
        hgain = W["a_head_gain"][L]
        maps = []
        for c in cores:
            b, h = c // 4, c % 4
            qp = np.zeros((192, SEQ + 3), np.float32)
            kp = np.zeros((192, SEQ + 3), np.float32)
            qp[:, 3:] = qkT[b][h * 192:(h + 1) * 192]
            kp[:, 3:] = qkT[b][768 + h * 192:768 + (h + 1) * 192]
            cwt = np.zeros((128, 4, 4), np.float32)
            cq = cw[:, h * 192:(h + 1) * 192].T
            ck = cw[:, 768 + h * 192:768 + (h + 1) * 192].T
            cwt[:, 0], cwt[:64, 1], cwt[:, 2], cwt[:64, 3] = cq[:128], cq[128:], ck[:128], ck[128:]
            m = {"masks": masks, "QP": qp, "KP": kp,
                 "V": np.ascontiguousarray(vT[b][h * 384:(h + 1) * 384].T)[None],
                 "IG": np.ascontiguousarray(gT[b][h].reshape(64, 128).T),
                 "FG": np.ascontiguousarray(gT[b][4 + h].reshape(64, 128).T),
                 "CW": cwt, "HG": colT(hgain[h * 384:(h + 1) * 384]),
                 "IDN": np.eye(128, dtype=np.float32), "LTRI": np.triu(np.ones((128, 128), np.float32)),
                 "USTR": np.triu(np.ones((64, 64), np.float32), 1)}
            maps.append(m)
        rr = _run(getprog("mlstm", lambda: seqmix_program("mlstm")), maps)
        main = []
        for c in cores:
            b, q = bq[c]
            main.append(np.concatenate([rr[b * 4 + h]["OT"][0][:, q * NTOK:(q + 1) * NTOK] for h in range(4)], axis=0))
        return main

    def mla_layer(r, KTf, Vf):
        QTf = [np.concatenate([r[b * 4 + q]["QT"] for q in range(4)], axis=2) for b in range(2)]
        maps = []
        for c in cores:
            b, g = c // 4, c % 4
            hs = slice(g * 3, g * 3 + 3)
            maps.append({"masks": masks, "QT": QTf[b][hs], "KT": KTf[b][hs], "V": Vf[b][hs]})
        rr = _run(getprog("mla", lambda: seqmix_program("mla")), maps)
        main = []
        for c in cores:
            b, q = bq[c]
            main.append(np.concatenate([rr[b * 4 + hd // 3]["OT"][hd % 3][:, q * NTOK:(q + 1) * NTOK] for hd in range(12)], axis=0))
        return main

    if _debug is not None and "main0" in _debug:
        main = _debug["main0"]
    else:
        main = mlstm_layer(r, 0)
    if _debug is not None:
        _debug["main0"] = main
        if _debug.get("stop") == "H0":
            return None
    maps = []
    for c in cores:
        m = {"xin": xcur[c], "mainT": main[c], "memoT_in": r[c]["memoT"], "oT_in": r[c]["oT"], "w_out": W["w_out"][0]}
        ffn_in(m, "fa", "ffn2", 0)
        ffn_in(m, "fb", "ffn1", 1)
        mix_in(m, c, 1, "A")
        maps.append(m)
    if _debug is not None and "T1" in _debug:
        r = _debug["T1"]
    else:
        r = _run(getprog("T1", lambda: tprog(1)), maps)
    xcur = [r[c]["xs"] for c in cores]
    if _debug is not None:
        _debug["T1"] = r
        if _debug.get("stop") == "T1":
            return None
    main = mlstm_layer(r, 1)
    maps = []
    for c in cores:
        m = {"xin": xcur[c], "mainT": main[c], "memoT_in": r[c]["memoT"], "oT_in": r[c]["oT"], "w_out": W["w_out"][1]}
        ffn_in(m, "fa", "ffn2", 1)
        ffn_in(m, "fb", "ffn1", 2)
        mix_in(m, c, 2, "B")
        m["g_kv_gain"] = colT(W["kv_gain"])
        m["g_kv_latent_gain"] = colT(W["kv_latent_gain"])
        kg = W["k_gain"]
        m["g_k_gain"] = pad_cols([kg[0:128], kg[128:192], np.concatenate([kg[160:192], kg[128:160]])])
        m["w_dkv"], m["w_ukv"] = W["w_dkv"], W["w_ukv"]
        m["posb"], m["invf"] = posb[c], invf
        maps.append(m)
    r = _run(tprog(2), maps)
    xcur = [r[c]["xs"] for c in cores]
    if _debug is not None:
        _debug["T2"] = r
        if _debug.get("stop") == "T2":
            return None
    KTf = [np.concatenate([r[b * 4 + q]["KT"] for q in range(4)], axis=2) for b in range(2)]
    Vf = [np.ascontiguousarray(np.concatenate([r[b * 4 + q]["VT"] for q in range(4)], axis=2).transpose(0, 2, 1)) for b in range(2)]
    main = mla_layer(r, KTf, Vf)
    maps = []
    for c in cores:
        m = {"xin": xcur[c], "mainT": main[c], "memoT_in": r[c]["memoT"], "w_out": W["w_out"][2]}
        ffn_in(m, "fa", "ffn2", 2)
        ffn_in(m, "fb", "ffn1", 3)
        mix_in(m, c, 3, "B")
        maps.append(m)
    r = _run(tprog(3), maps)
    xcur = [r[c]["xs"] for c in cores]
    main = mla_layer(r, KTf, Vf)
    maps = []
    for c in cores:
        m = {"xin": xcur[c], "mainT": main[c], "memoT_in": r[c]["memoT"], "w_out": W["w_out"][3]}
        ffn_in(m, "fa", "ffn2", 3)
        maps.append(m)
    r = _run(tprog(4), maps)
    out = np.zeros((2, SEQ, D), np.float32)
    for c in cores:
        out[bq[c][0], tok(c)] = r[c]["xs"].T
    return out
```
